# Optimizing a Trainium2 kernel written in Bass

```python
import math
import jax, jax.numpy as jnp
from jax import lax
import numpy as np

D_MODEL = 1024
BATCH = 8
SEQ = 4096
DEPTH = 2

LRU_WIDTH = D_MODEL
LRU_BLOCKS = 4
LRU_BLOCK_W = LRU_WIDTH // LRU_BLOCKS
RG_LRU_C = 8.0
CONV_WIDTH = 4
CONV_PAD_LEFT = 2
CONV_PAD_RIGHT = CONV_WIDTH - 1 - CONV_PAD_LEFT

RET_HEADS = 4
RET_QK_DIM = D_MODEL
RET_V_DIM = 2 * D_MODEL
RET_HEAD_QK = RET_QK_DIM // RET_HEADS
RET_HEAD_V = RET_V_DIM // RET_HEADS
RET_CHUNK = 128
ROPE_BASE = 10000.0

FFN_HIDDEN = ((8 * D_MODEL // 3 + 255) // 256) * 256

NORM_EPS = 1e-6
N_LRU_LAYERS = (DEPTH + 1) // 2
N_RET_LAYERS = DEPTH // 2

kernel_name = "bidir_hybrid_rglru_retention_swiglu"


def rms_norm(x, g):
    xf = x.astype(jnp.float32)
    y = xf * lax.rsqrt(jnp.mean(xf * xf, axis=-1, keepdims=True) + NORM_EPS)
    return (y * g.astype(jnp.float32)).astype(x.dtype)


def _linear_scan_combine(left, right):
    a1, b1 = left
    a2, b2 = right
    return a1 * a2, a2 * b1 + b2


def rg_lru_mixer(h, w_in, conv_w, conv_b, ga_w, ga_b, gx_w, gx_b, lam, w_out):
    b, s, _ = h.shape
    u = h @ w_in
    xb, gb = jnp.split(u, 2, axis=-1)
    xb = lax.conv_general_dilated(
        xb, conv_w[:, None, :].astype(xb.dtype), window_strides=(1,),
        padding=[(CONV_PAD_LEFT, CONV_PAD_RIGHT)],
        dimension_numbers=("NWC", "WIO", "NWC"),
        feature_group_count=LRU_WIDTH) + conv_b
    xg = xb.reshape(b, s, LRU_BLOCKS, LRU_BLOCK_W)
    h_sum = jnp.zeros((b, s, LRU_WIDTH), jnp.float32)
    for d in range(2):
        r = jax.nn.sigmoid(
            jnp.einsum("bsnk,nkj->bsnj", xg, ga_w[d]).reshape(b, s, LRU_WIDTH) + ga_b[d])
        i = jax.nn.sigmoid(
            jnp.einsum("bsnk,nkj->bsnj", xg, gx_w[d]).reshape(b, s, LRU_WIDTH) + gx_b[d])
        log_a = (-RG_LRU_C * r.astype(jnp.float32)) * jax.nn.softplus(-lam[d].astype(jnp.float32))
        a = jnp.exp(log_a)
        inp = jnp.sqrt(-jnp.expm1(2.0 * log_a)) * (i * xb).astype(jnp.float32)
        if d == 1:
            a, inp = jnp.flip(a, axis=1), jnp.flip(inp, axis=1)
        _, hs = lax.associative_scan(_linear_scan_combine, (a, inp), axis=1)
        if d == 1:
            hs = jnp.flip(hs, axis=1)
        h_sum = h_sum + hs
    y = h_sum.astype(h.dtype) * jax.nn.gelu(gb, approximate=True)
    return y @ w_out


def apply_rotary(x):
    s, d = x.shape[1], x.shape[-1]
    half = d // 2
    inv_freq = 1.0 / (ROPE_BASE ** jnp.linspace(0.0, 1.0, half, dtype=jnp.float32))
    ang = jnp.arange(s, dtype=jnp.float32)[:, None] * inv_freq[None, :]
    cos = jnp.cos(ang)[None, :, None, :].astype(x.dtype)
    sin = jnp.sin(ang)[None, :, None, :].astype(x.dtype)
    x1, x2 = x[..., :half], x[..., half:]
    return jnp.concatenate([x1 * cos - x2 * sin, x2 * cos + x1 * sin], axis=-1)


def chunk_retention(q, k, v, log_g, strict):
    b, nh, s, dk = q.shape
    dv = v.shape[-1]
    nc = s // RET_CHUNK
    qc = q.reshape(b, nh, nc, RET_CHUNK, dk)
    kc = k.reshape(b, nh, nc, RET_CHUNK, dk)
    vc = v.reshape(b, nh, nc, RET_CHUNK, dv)
    pos = jnp.arange(RET_CHUNK, dtype=jnp.float32)
    diff = pos[:, None] - pos[None, :]
    mask = diff > 0 if strict else diff >= 0
    dec = jnp.where(mask[None], jnp.exp(jnp.where(mask, diff, 0.0)[None] * log_g[:, None, None]), 0.0)
    scores = jnp.einsum("bhnid,bhnjd->bhnij", qc, kc) * dec[None, :, None].astype(q.dtype)
    y_intra = jnp.einsum("bhnij,bhnje->bhnie", scores, vc)
    zeta = jnp.exp((RET_CHUNK - 1.0 - pos)[None, :] * log_g[:, None]).astype(q.dtype)
    xi = jnp.exp((pos + 1.0)[None, :] * log_g[:, None]).astype(q.dtype)
    g_chunk = jnp.exp(RET_CHUNK * log_g).astype(q.dtype)

    def step(state, inp):
        q_n, k_n, v_n = inp
        y = jnp.einsum("bhid,bhde->bhie", q_n, state) * xi[None, :, :, None]
        state = state * g_chunk[None, :, None, None] + jnp.einsum(
            "bhjd,bhje->bhde", k_n * zeta[None, :, :, None], v_n)
        return state, y

    state0 = jnp.zeros((b, nh, dk, dv), q.dtype)
    xs = (jnp.moveaxis(qc, 2, 0), jnp.moveaxis(kc, 2, 0), jnp.moveaxis(vc, 2, 0))
    _, ys = lax.scan(step, state0, xs)
    y = y_intra + jnp.moveaxis(ys, 0, 2).astype(y_intra.dtype)
    return y.reshape(b, nh, s, dv)


def retention_mixer(h, w_in, w_out):
    b, s, _ = h.shape
    u = h @ w_in
    q, k, v, gate = jnp.split(u, [RET_QK_DIM, 2 * RET_QK_DIM, 2 * RET_QK_DIM + RET_V_DIM], axis=-1)
    q = apply_rotary(q.reshape(b, s, RET_HEADS, RET_HEAD_QK))
    k = apply_rotary(k.reshape(b, s, RET_HEADS, RET_HEAD_QK)) * (RET_HEAD_QK ** -0.5)
    v = v.reshape(b, s, RET_HEADS, RET_HEAD_V)
    q, k, v = (jnp.transpose(t, (0, 2, 1, 3)) for t in (q, k, v))
    log_g_fwd = jnp.log1p(-jnp.exp2(-5.0 - jnp.arange(RET_HEADS, dtype=jnp.float32)))
    log_g_bwd = log_g_fwd[::-1]
    y_f = chunk_retention(q, k, v, log_g_fwd, strict=False)
    y_b = jnp.flip(chunk_retention(jnp.flip(q, 2), jnp.flip(k, 2), jnp.flip(v, 2),
                                   log_g_bwd, strict=True), 2)
    y = (y_f + y_b).astype(jnp.float32)
    y = y * lax.rsqrt(jnp.mean(y * y, axis=-1, keepdims=True) + NORM_EPS)
    y = jnp.transpose(y, (0, 2, 1, 3)).reshape(b, s, RET_V_DIM).astype(h.dtype)
    return (y * jax.nn.silu(gate)) @ w_out


def swiglu_ffn(h, w_gate, w_up, w_down):
    return (jax.nn.silu(h @ w_gate) * (h @ w_up)) @ w_down


def setup_inputs(seed: int = 0) -> dict:
    key = jax.random.key(seed)
    ks = iter(jax.random.split(key, 32))
    f32 = jnp.float32

    def nrm(shape, fan_in):
        return jax.random.normal(next(ks), shape, f32) * (fan_in ** -0.5)

    def small(shape):
        return 0.01 * jax.random.normal(next(ks), shape, f32)

    a_c = jax.random.uniform(next(ks), (N_LRU_LAYERS, 2, LRU_WIDTH), f32, 0.9, 0.999)
    sig = a_c ** (1.0 / RG_LRU_C)
    lam = jnp.log(sig) - jnp.log1p(-sig)
    return {
        "x": jax.random.normal(next(ks), (BATCH, SEQ, D_MODEL), f32),
        "ln_mix": 1.0 + small((DEPTH, D_MODEL)),
        "ln_ffn": 1.0 + small((DEPTH, D_MODEL)),
        "ln_final": 1.0 + small((D_MODEL,)),
        "lru_w_in": nrm((N_LRU_LAYERS, D_MODEL, 2 * LRU_WIDTH), D_MODEL),
        "lru_conv_w": nrm((N_LRU_LAYERS, CONV_WIDTH, LRU_WIDTH), CONV_WIDTH),
        "lru_conv_b": small((N_LRU_LAYERS, LRU_WIDTH)),
        "lru_gate_a_w": nrm((N_LRU_LAYERS, 2, LRU_BLOCKS, LRU_BLOCK_W, LRU_BLOCK_W), LRU_BLOCK_W),
        "lru_gate_a_b": small((N_LRU_LAYERS, 2, LRU_WIDTH)),
        "lru_gate_x_w": nrm((N_LRU_LAYERS, 2, LRU_BLOCKS, LRU_BLOCK_W, LRU_BLOCK_W), LRU_BLOCK_W),
        "lru_gate_x_b": small((N_LRU_LAYERS, 2, LRU_WIDTH)),
        "lru_lambda": lam,
        "lru_w_out": nrm((N_LRU_LAYERS, LRU_WIDTH, D_MODEL), LRU_WIDTH),
        "ret_w_in": nrm((N_RET_LAYERS, D_MODEL, 2 * RET_QK_DIM + 2 * RET_V_DIM), D_MODEL),
        "ret_w_out": nrm((N_RET_LAYERS, RET_V_DIM, D_MODEL), RET_V_DIM),
        "ffn_w_gate": nrm((DEPTH, D_MODEL, FFN_HIDDEN), D_MODEL),
        "ffn_w_up": nrm((DEPTH, D_MODEL, FFN_HIDDEN), D_MODEL),
        "ffn_w_down": nrm((DEPTH, FFN_HIDDEN, D_MODEL), FFN_HIDDEN),
    }


def reference(x, ln_mix, ln_ffn, ln_final, lru_w_in, lru_conv_w, lru_conv_b,
              lru_gate_a_w, lru_gate_a_b, lru_gate_x_w, lru_gate_x_b, lru_lambda,
              lru_w_out, ret_w_in, ret_w_out, ffn_w_gate, ffn_w_up, ffn_w_down):
    h = x
    for layer in range(DEPTH):
        j = layer // 2
        hn = rms_norm(h, ln_mix[layer])
        if layer % 2 == 0:
            mix = rg_lru_mixer(hn, lru_w_in[j], lru_conv_w[j], lru_conv_b[j],
                               lru_gate_a_w[j], lru_gate_a_b[j], lru_gate_x_w[j],
                               lru_gate_x_b[j], lru_lambda[j], lru_w_out[j])
        else:
            mix = retention_mixer(hn, ret_w_in[j], ret_w_out[j])
        h = h + mix.astype(h.dtype)
        h = h + swiglu_ffn(rms_norm(h, ln_ffn[layer]), ffn_w_gate[layer],
                           ffn_w_up[layer], ffn_w_down[layer]).astype(h.dtype)
    return rms_norm(h, ln_final)
```

```python
import contextlib
import math
import numpy as np
import concourse.bass as bass
import concourse.mybir as mybir
from concourse.bass_utils import run_bass_kernel_spmd

F32 = mybir.dt.float32
BF16 = mybir.dt.bfloat16
AF = mybir.ActivationFunctionType
ALU = mybir.AluOpType

D = 1024
SEQ = 4096
NCORES = 8
FH = 2816
NM = FH // 128
EPS = 1e-6
NPOOL = 12
T = 256
NS = T // 128
NT = SEQ // T
NCH = SEQ // 128


class Buf:
    __slots__ = ("name", "ws", "rs")

    def __init__(self, name="b"):
        self.name = name
        self.ws = []
        self.rs = []


class Op:
    __slots__ = ("eng", "fn", "deps", "sig", "is_dma", "signaled", "prev")

    def __init__(self, eng, fn, is_dma):
        self.eng = eng
        self.fn = fn
        self.is_dma = is_dma
        self.deps = []
        self.sig = None
        self.signaled = is_dma
        self.prev = None


class Sched:
    ENGS = ("pe", "act", "dve", "pool", "sp")

    def __init__(self, nc):
        self.nc = nc
        self.ops = {e: [] for e in self.ENGS}
        self.last = {}
        self.dmas = []
        self.pending = {}

    def _deps(self, o, reads, writes):
        deps = list(self.pending.pop(o.eng, []))
        for b in reads:
            deps.extend(b.ws)
        for b in writes:
            if o.is_dma and b.ws and not b.rs and all(w.is_dma for w in b.ws):
                continue
            for w in b.ws + b.rs:
                same = (not o.is_dma) and (not w.is_dma) and w.eng == o.eng
                if not same:
                    deps.append(w)
        seen = set()
        for d in deps:
            if id(d) not in seen and d is not o:
                seen.add(id(d))
                d.signaled = True
                o.deps.append(d)
        for b in reads:
            b.rs.append(o)
        for b in writes:
            if o.is_dma and b.ws and not b.rs and all(w.is_dma for w in b.ws):
                b.ws.append(o)
            else:
                b.ws = [o]
                b.rs = []

    def op(self, eng, fn, reads=(), writes=()):
        o = Op(eng, fn, False)
        self._deps(o, reads, writes)
        self.ops[eng].append(o)
        self.last[eng] = o
        return o

    def dma(self, q, out, in_, reads=(), writes=(), slow=False):
        if slow:
            fn = lambda e: e.dma_start(out=out, in_=in_, allow_slow_non_contiguous=True)
        else:
            fn = lambda e: e.dma_start(out=out, in_=in_)
        o = Op(q, fn, True)
        self._deps(o, reads, writes)
        self.ops[q].append(o)
        self.dmas.append(o)
        return o

    def barrier(self):
        deps = [o for o in self.last.values()] + list(self.dmas)
        for d in deps:
            d.signaled = True
        for e in self.ENGS:
            self.pending[e] = list(self.pending.get(e, [])) + [d for d in deps if not (d.eng == e and not d.is_dma)]
        self.dmas = []

    def emit(self):
        nc = self.nc
        sems = {}
        with contextlib.ExitStack() as st:
            for e in ("pe", "act", "dve", "pool"):
                sems[e] = st.enter_context(nc.semaphore("c_" + e))
            pools = {}
            for q in ("sp", "act", "pool"):
                if any(o.is_dma for o in self.ops[q]):
                    pools[q] = [st.enter_context(nc.semaphore("d_%s%d" % (q, i))) for i in range(NPOOL)]
            for e in self.ENGS:
                cnt = 0
                j = 0
                for o in self.ops[e]:
                    if o.is_dma:
                        s = pools[e][j % NPOOL]
                        if j >= NPOOL:
                            o.prev = (s, 16 * (j // NPOOL))
                        o.sig = (s, 16 * (j // NPOOL + 1))
                        j += 1
                    elif o.signaled:
                        cnt += 1
                        o.sig = (sems[e], cnt)
            block = st.enter_context(nc.Block())

            def run(e, eng):
                known = {}
                finals = {}
                for o in self.ops[e]:
                    waits = {}
                    cands = [d.sig for d in o.deps]
                    if o.prev is not None:
                        cands.append(o.prev)
                    for s, v in cands:
                        k = id(s)
                        if k not in waits or waits[k][1] < v:
                            waits[k] = (s, v)
                    for k, (s, v) in waits.items():
                        if known.get(k, 0) < v:
                            eng.wait_ge(s, v)
                            known[k] = v
                    inst = o.fn(eng)
                    if o.is_dma:
                        inst.then_inc(o.sig[0], 16)
                        finals[id(o.sig[0])] = o.sig
                    elif o.signaled:
                        inst.then_inc(o.sig[0], 1)
                for s, v in finals.values():
                    eng.wait_ge(s, v)

            @block.tensor
            def _(eng):
                run("pe", eng)

            @block.scalar
            def _(eng):
                run("act", eng)

            @block.vector
            def _(eng):
                run("dve", eng)

            @block.gpsimd
            def _(eng):
                run("pool", eng)

            @block.sync
            def _(eng):
                run("sp", eng)


class Arena:
    def __init__(self, nc, words, base_name="arena"):
        self.t = nc.alloc_sbuf_tensor(base_name, [128, words], F32)
        self.words = words
        self.off = 0
        self.mark = 0

    def reset(self, to=None):
        self.off = self.mark if to is None else to

    def alloc(self, shape, dt):
        n = int(np.prod(shape))
        w = (n + 1) // 2 if dt == BF16 else n
        w = (w + 7) // 8 * 8
        assert self.off + w <= self.words, "SBUF arena overflow %d + %d > %d" % (self.off, w, self.words)
        ap = self.t[:, self.off:self.off + w]
        self.off += w
        if dt == BF16:
            ap = ap.bitcast(BF16)
        ap = ap[:, 0:n]
        if len(shape) == 2:
            ap = ap.rearrange("p (a b) -> p a b", b=shape[1])
        elif len(shape) == 3:
            ap = ap.rearrange("p (a b c) -> p a b c", b=shape[1], c=shape[2])
        return ap, Buf()


def _gammas():
    gf = [1.0 - 2.0 ** (-5.0 - h) for h in range(4)]
    gb = gf[::-1]
    return gf, gb


def host_tables():
    half = 128
    inv_freq = (1.0 / (np.float32(10000.0) ** np.linspace(0.0, 1.0, half, dtype=np.float32))).astype(np.float32)
    ang = (np.arange(SEQ, dtype=np.float32)[:, None] * inv_freq[None, :]).astype(np.float32)
    cosT = np.ascontiguousarray(np.cos(ang.astype(np.float64)).T).astype(np.float32)
    sinT = np.ascontiguousarray(np.sin(ang.astype(np.float64)).T).astype(np.float32)
    gf, gb = _gammas()
    pos = np.arange(128, dtype=np.float64)
    MT = np.zeros((4, 128, 128), np.float64)
    XF = np.zeros((4, 128, 128), np.float64)
    XB = np.zeros((4, 128, 128), np.float64)
    ZF = np.zeros((128, 4), np.float64)
    ZB = np.zeros((128, 4), np.float64)
    for h in range(4):
        i = pos[None, :]
        j = pos[:, None]
        fwd = np.where(i >= j, gf[h] ** np.maximum(i - j, 0), 0.0)
        bwd = np.where(j > i, gb[h] ** np.maximum(j - i, 0), 0.0)
        MT[h] = (fwd + bwd) / 16.0
        XF[h] = np.broadcast_to(gf[h] ** (pos + 1.0), (128, 128))
        XB[h] = np.broadcast_to(gb[h] ** (128.0 - pos), (128, 128))
        ZF[:, h] = gf[h] ** (127.0 - pos) / 16.0
        ZB[:, h] = gb[h] ** pos / 16.0
    tabs = np.concatenate([MT, XF, XB], axis=0).astype(np.float32)
    tabs = np.ascontiguousarray(tabs.transpose(1, 0, 2))
    zz = np.concatenate([ZF, ZB], axis=1).astype(np.float32)
    return cosT, sinT, tabs, zz


class Prog:
    def __init__(self, dbg=False, phases=None):
        self.dbg = dbg
        self.phases = phases
        nc = self.nc = bass.Bass("TRN2", target_bir_lowering=False)
        self.S = Sched(nc)
        I = lambda name, shape: nc.dram_tensor(name, shape, F32, kind="ExternalInput").ap()
        self.x = I("x", [SEQ, D])
        self.ln_mix = I("ln_mix", [2, D])
        self.ln_ffn = I("ln_ffn", [2, D])
        self.ln_final = I("ln_final", [D])
        self.lru_w_in = I("lru_w_in", [D, 2 * D])
        self.lru_conv_w = I("lru_conv_w", [4, D])
        self.lru_conv_b = I("lru_conv_b", [D])
        self.lru_ga_w = I("lru_gate_a_w", [2, 4, 256, 256])
        self.lru_ga_b = I("lru_gate_a_b", [2, D])
        self.lru_gx_w = I("lru_gate_x_w", [2, 4, 256, 256])
        self.lru_gx_b = I("lru_gate_x_b", [2, D])
        self.lru_lam = I("lru_lambda", [2, D])
        self.lru_w_out = I("lru_w_out", [D, D])
        self.ret_w_in = I("ret_w_in", [D, 6 * D])
        self.ret_w_out = I("ret_w_out", [2 * D, D])
        self.ffn_wg = I("ffn_w_gate", [2, D, FH])
        self.ffn_wu = I("ffn_w_up", [2, D, FH])
        self.ffn_wd = I("ffn_w_down", [2, FH, D])
        self.cosT_d = I("cosT", [128, SEQ])
        self.sinT_d = I("sinT", [128, SEQ])
        self.tabs_d = I("tabs", [128, 12, 128])
        self.zz_d = I("zz", [128, 8])
        kind = "ExternalOutput" if dbg else "Internal"
        self.hsb = nc.dram_tensor("hsb", [D, SEQ], F32, kind=kind).ap()
        self.hm0 = nc.dram_tensor("hm0", [SEQ, D], F32, kind=kind).ap()
        self.h1 = nc.dram_tensor("h1", [SEQ, D], F32, kind=kind).ap()
        self.hm1 = nc.dram_tensor("hm1", [SEQ, D], F32, kind=kind).ap()
        self.sbst = nc.dram_tensor("sbst", [NCH, 128, 8, 512], BF16, kind="Internal").ap()
        self.hnT0d = nc.dram_tensor("hnT0d", [128, 8, SEQ], BF16, kind="Internal").ap()
        self.hnT1d = nc.dram_tensor("hnT1d", [128, 8, SEQ], BF16, kind="Internal").ap()
        self.xbd = nc.dram_tensor("xbd", [D, SEQ], F32, kind="Internal").ap()
        self.kTd = nc.dram_tensor("kTd", [128, 8, SEQ], BF16, kind="Internal").ap()
        self.xbbd = nc.dram_tensor("xbbd", [D, SEQ], BF16, kind="Internal").ap()
        self.vd = nc.dram_tensor("vd", [SEQ, 2 * D], BF16, kind="Internal").ap()
        self.out = nc.dram_tensor("out", [SEQ, D], F32, kind="ExternalOutput").ap()
        self.dbuf = {k: [Buf() for _ in range(NT)] for k in ("hsb", "hm0", "h1", "hm1", "out", "hnT0", "hnT1", "xb", "kT", "v", "xbb")}
        self.dsb = [Buf() for _ in range(NCH)]
        self.ps = [nc.alloc_psum_tensor("ps%d" % i, [128, 512], F32) for i in range(8)]
        self.pb = [Buf() for _ in range(8)]
        self.ar = Arena(nc, 53000)
        self._consts()

    def _consts(self):
        S, ar = self.S, self.ar
        identf, b0 = ar.alloc([128], F32)
        self.ident, self.b_ident = ar.alloc([128], BF16)
        self.mhalf, self.b_mhalf = ar.alloc([8], F32)
        S.op("pool", lambda e: e.memset(identf, 0.0), writes=[b0])
        S.op("pool", lambda e: e.affine_select(out=identf, in_=identf, pattern=[[-1, 128]], compare_op=ALU.not_equal,
                                               fill=1.0, base=0, channel_multiplier=1), reads=[b0], writes=[b0])
        S.op("dve", lambda e: e.tensor_copy(out=self.ident, in_=identf), reads=[b0], writes=[self.b_ident])
        S.op("pool", lambda e: e.memset(self.mhalf, -0.5), writes=[self.b_mhalf])
        ar.mark = ar.off

    def load_gain(self, src_row):
        g, b = self.ar.alloc([D], F32)
        self.S.dma("sp", g, src_row.partition_broadcast(128), writes=[b])
        return g, b

    def rstd_from(self, ssq, b_ssq, n, mult, add):
        S = self.S
        S.op("dve", lambda e: e.tensor_scalar(out=ssq[:, 0:n], in0=ssq[:, 0:n], scalar1=mult, scalar2=add,
                                              op0=ALU.mult, op1=ALU.add), reads=[b_ssq], writes=[b_ssq])
        S.op("pool", lambda e: e.tensor_tensor(out=ssq[:, 0:n], in0=ssq[:, 0:n], in1=self.mhalf[:, 0:n], op=ALU.pow),
             reads=[b_ssq, self.b_mhalf], writes=[b_ssq])

    def norm_T(self, xt, b_xt, g, b_g, hnT, b_hnT, st, banks=(6, 7)):
        self.norm_a(xt, b_xt, g, b_g, st)
        self.norm_b(hnT, b_hnT, st, banks)

    def norm_a(self, xt, b_xt, g, b_g, st):
        S = self.S
        ssq, b_ssq = st["ssq"], st["b_ssq"]
        for s in range(NS):
            S.op("act", lambda e, s=s: e.activation(out=st["junk"], in_=xt[:, s, :], func=AF.Square,
                                                   accum_out=ssq[:, s:s + 1]),
                 reads=[b_xt], writes=[st["b_junk"], b_ssq])
        self.rstd_from(ssq, b_ssq, NS, 1.0 / D, EPS)
        for s in range(NS):
            hn, b_hn = st["hn"][s % 2]
            S.op("dve", lambda e, s=s, hn=hn: e.scalar_tensor_tensor(out=hn, in0=xt[:, s, :], scalar=ssq[:, s:s + 1],
                                                                  in1=g, op0=ALU.mult, op1=ALU.mult),
                 reads=[b_xt, b_ssq, b_g], writes=[b_hn])

    def norm_b(self, hnT, b_hnT, st, banks=(6, 7)):
        S = self.S
        for s in range(NS):
            hn, b_hn = st["hn"][s % 2]
            bank = banks[s % 2]
            psb = self.ps[bank][:].bitcast(BF16)
            for kc in range(8):
                S.op("pe", lambda e, kc=kc, hn=hn, psb=psb: e.transpose(out=psb[:, kc * 128:(kc + 1) * 128],
                                                                        in_=hn[:, kc * 128:(kc + 1) * 128],
                                                                        identity=self.ident),
                     reads=[b_hn, self.b_ident], writes=[self.pb[bank]])
            src = psb.rearrange("p (k t) -> p k t", k=8)
            dst = hnT[:, :, s * 128:(s + 1) * 128]
            if s % 2 == 0:
                S.op("act", lambda e, src=src, dst=dst: e.activation(out=dst, in_=src, func=AF.Copy),
                     reads=[self.pb[bank]], writes=[b_hnT])
            else:
                S.op("dve", lambda e, src=src, dst=dst: e.tensor_copy(out=dst, in_=src),
                     reads=[self.pb[bank]], writes=[b_hnT])

    def norm_state(self):
        ar = self.ar
        st = {}
        st["ssq"], st["b_ssq"] = ar.alloc([8], F32)
        st["junk"], st["b_junk"] = ar.alloc([D], BF16)
        st["hn"] = [ar.alloc([D], BF16) for _ in range(2)]
        return st

    def load_w(self, dst, b_dst, src2d, k_chunks, col0, ncols, piece):
        v = src2d.rearrange("(k p) n -> p k n", p=128)
        for c in range(0, ncols, piece):
            self.S.dma("pool", dst[:, :, c:c + piece], v[:, 0:k_chunks, col0 + c:col0 + c + piece], writes=[b_dst])

    def mm_proj(self, bank, w, b_w, m0, hnT, b_hnT, n=T):
        for kc in range(8):
            self.S.op("pe", lambda e, kc=kc: e.matmul(self.ps[bank][:, 0:n], lhsT=w[:, kc, m0:m0 + 128],
                                                     rhs=hnT[:, kc, 0:n], start=(kc == 0), stop=(kc == 7)),
                      reads=[b_w, b_hnT], writes=[self.pb[bank]])

    def mm_tok(self, bank, act, b_act, s, w, b_w, c0, nk):
        for k in range(nk):
            self.S.op("pe", lambda e, k=k: e.matmul(self.ps[bank][:, 0:512], lhsT=act[:, k, s * 128:(s + 1) * 128],
                                                   rhs=w[:, k, c0:c0 + 512], start=(k == 0), stop=(k == nk - 1)),
                      reads=[b_act, b_w], writes=[self.pb[bank]])

    def phase_ffn(self, layer, src, src_key, dst, dst_key, final):
        S, ar = self.S, self.ar
        S.barrier()
        ar.reset()
        wg, b_wg = ar.alloc([8, FH], BF16)
        wu, b_wu = ar.alloc([8, FH], BF16)
        wd, b_wd = ar.alloc([NM, D], BF16)
        g, b_g = self.load_gain(self.ln_ffn[layer])
        if final:
            gf, b_gf = self.load_gain(self.ln_final)
        b_wgm = [Buf() for _ in range(NM // 2)]
        b_wum = [Buf() for _ in range(NM // 2)]
        vg = self.ffn_wg[layer].rearrange("(k p) n -> p k n", p=128)
        vu = self.ffn_wu[layer].rearrange("(k p) n -> p k n", p=128)
        for mg in range(NM // 2):
            S.dma("pool", wg[:, :, mg * 256:(mg + 1) * 256], vg[:, :, mg * 256:(mg + 1) * 256], writes=[b_wgm[mg]])
            S.dma("pool", wu[:, :, mg * 256:(mg + 1) * 256], vu[:, :, mg * 256:(mg + 1) * 256], writes=[b_wum[mg]])
        vd = self.ffn_wd[layer].rearrange("(k p) n -> p k n", p=128)
        S.dma("pool", wd[:, 0:11, :], vd[:, 0:11, :], writes=[b_wd])
        S.dma("pool", wd[:, 11:22, :], vd[:, 11:22, :], writes=[b_wd])
        st = self.norm_state()
        xts = [ar.alloc([NS, D], F32) for _ in range(2)]
        hnTs = [ar.alloc([8, T], BF16) for _ in range(2)]
        act, b_act = ar.alloc([NM, T], BF16)
        tt = [ar.alloc([T], F32) for _ in range(2)]
        sg = [ar.alloc([T], F32) for _ in range(2)]
        def load_x(j):
            S.dma("sp", xts[j % 2][0], src[j * T:(j + 1) * T, :].rearrange("(s p) d -> p s d", p=128),
                  reads=[self.dbuf[src_key][j]], writes=[xts[j % 2][1]])

        load_x(0)
        self.norm_T(xts[0][0], xts[0][1], g, b_g, hnTs[0][0], hnTs[0][1], st)
        for i in range(NT):
            self.ffn_tile(i, xts, hnTs, st, g, b_g, load_x, locals())

    def ffn_tile(self, i, xts, hnTs, st, g, b_g, load_x, env):
        S = self.S
        wg, wu, wd, b_wgm, b_wum, b_wd = env["wg"], env["wu"], env["wd"], env["b_wgm"], env["b_wum"], env["b_wd"]
        act, b_act, tt, sg, final = env["act"], env["b_act"], env["tt"], env["sg"], env["final"]
        dst, dst_key = env["dst"], env["dst_key"]
        if final:
            gf, b_gf = env["gf"], env["b_gf"]
        if True:
            xt, b_xt = xts[i % 2]
            hnT, b_hnT = hnTs[i % 2]
            if i + 1 < NT:
                load_x(i + 1)
            for m in range(NM):
                if m == 12 and i + 1 < NT:
                    self.norm_a(xts[(i + 1) % 2][0], xts[(i + 1) % 2][1], g, b_g, st)
                bg, bu = (m % 2) * 2, (m % 2) * 2 + 1
                self.mm_proj(bg, wg, b_wgm[m // 2], m * 128, hnT, b_hnT)
                self.mm_proj(bu, wu, b_wum[m // 2], m * 128, hnT, b_hnT)
                t_, b_t = tt[m % 2]
                s_, b_s = sg[m % 2]
                pg, pu = self.ps[bg][:, 0:T], self.ps[bu][:, 0:T]
                S.op("act", lambda e, t_=t_, pg=pg: e.activation(out=t_, in_=pg, func=AF.Tanh, scale=0.5),
                     reads=[self.pb[bg]], writes=[b_t])
                S.op("dve", lambda e, t_=t_, s_=s_, pg=pg: e.scalar_tensor_tensor(out=s_, in0=t_, scalar=1.0, in1=pg,
                                                                                 op0=ALU.add, op1=ALU.mult),
                     reads=[b_t, self.pb[bg]], writes=[b_s])
                S.op("dve", lambda e, s_=s_, pu=pu, m=m: e.scalar_tensor_tensor(out=act[:, m, :], in0=s_, scalar=0.5,
                                                                               in1=pu, op0=ALU.mult, op1=ALU.mult),
                     reads=[b_s, self.pb[bu]], writes=[b_act])
            for s in range(NS):
                for n in range(2):
                    bank = 4 + ((s * 2 + n) % 2)
                    self.mm_tok(bank, act, b_act, s, wd, b_wd, n * 512, NM)
                    xs = xt[:, s, n * 512:(n + 1) * 512]
                    S.op("dve", lambda e, xs=xs, bank=bank: e.tensor_tensor(out=xs, in0=xs, in1=self.ps[bank][:, 0:512],
                                                                           op=ALU.add),
                         reads=[self.pb[bank]], writes=[b_xt])
            if final:
                ssq, b_ssq = st["ssq"], st["b_ssq"]
                for s in range(NS):
                    S.op("act", lambda e, s=s, xt=xt: e.activation(out=st["junk"], in_=xt[:, s, :], func=AF.Square,
                                                                  accum_out=ssq[:, 4 + s:5 + s]),
                         reads=[b_xt], writes=[st["b_junk"], b_ssq])
                S.op("dve", lambda e: e.tensor_scalar(out=ssq[:, 4:4 + NS], in0=ssq[:, 4:4 + NS], scalar1=1.0 / D,
                                                      scalar2=EPS, op0=ALU.mult, op1=ALU.add),
                     reads=[b_ssq], writes=[b_ssq])
                S.op("pool", lambda e: e.tensor_tensor(out=ssq[:, 4:4 + NS], in0=ssq[:, 4:4 + NS],
                                                       in1=self.mhalf[:, 0:NS], op=ALU.pow),
                     reads=[b_ssq, self.b_mhalf], writes=[b_ssq])
                for s in range(NS):
                    S.op("dve", lambda e, s=s, xt=xt: e.scalar_tensor_tensor(out=xt[:, s, :], in0=xt[:, s, :],
                                                                     scalar=ssq[:, 4 + s:5 + s], in1=gf,
                                                                     op0=ALU.mult, op1=ALU.mult),
                         reads=[b_ssq, b_gf], writes=[b_xt])
            S.dma("sp", dst[i * T:(i + 1) * T, :].rearrange("(s p) d -> p s d", p=128), xt,
                  reads=[b_xt], writes=[self.dbuf[dst_key][i]])
            if i + 1 < NT:
                self.norm_b(hnTs[(i + 1) % 2][0], hnTs[(i + 1) % 2][1], st)

    def load_chanvec(self, dst, b, src1d):
        self.S.dma("sp", dst, src1d.rearrange("(c p) -> p c", p=128), writes=[b], slow=True)

    def phase_lru(self, bwd):
        S, ar = self.S, self.ar
        S.barrier()
        ar.reset()
        d = 1 if bwd else 0
        ps, pb = self.ps, self.pb
        ga, b_ga = ar.alloc([4, 2, 256], BF16)
        gx, b_gx = ar.alloc([4, 2, 256], BF16)
        S.dma("pool", ga, self.lru_ga_w[d].rearrange("n (k p) j -> p n k j", p=128), writes=[b_ga])
        S.dma("pool", gx, self.lru_gx_w[d].rearrange("n (k p) j -> p n k j", p=128), writes=[b_gx])
        if bwd:
            wx, b_wx = ar.alloc([8, D], BF16)
            self.load_w(wx, b_wx, self.lru_w_in, 8, 0, D, 256)
            g, b_g = self.load_gain(self.ln_mix[0])
            cw = [ar.alloc([8], F32) for _ in range(4)]
            for k in range(4):
                self.load_chanvec(cw[k][0], cw[k][1], self.lru_conv_w[k])
            cb, b_cb = ar.alloc([8], F32)
            self.load_chanvec(cb, b_cb, self.lru_conv_b)
        else:
            wgb, b_wgb = ar.alloc([8, D], BF16)
            self.load_w(wgb, b_wgb, self.lru_w_in, 8, D, D, 256)
            wo, b_wo = ar.alloc([8, D], BF16)
            self.load_w(wo, b_wo, self.lru_w_out, 8, 0, D, 512)
        hba, b_hba = ar.alloc([8], F32)
        hbx, b_hbx = ar.alloc([8], F32)
        lam, b_lam = ar.alloc([8], F32)
        self.load_chanvec(hba, b_hba, self.lru_ga_b[d])
        self.load_chanvec(hbx, b_hbx, self.lru_gx_b[d])
        self.load_chanvec(lam, b_lam, self.lru_lam[d])
        S.op("dve", lambda e: e.tensor_scalar(out=hba, in0=hba, scalar1=0.5, scalar2=None, op0=ALU.mult),
             reads=[b_hba], writes=[b_hba])
        S.op("dve", lambda e: e.tensor_scalar(out=hbx, in0=hbx, scalar1=0.5, scalar2=None, op0=ALU.mult),
             reads=[b_hbx], writes=[b_hbx])
        ee, b_ee = ar.alloc([8], F32)
        pp, b_pp = ar.alloc([8], F32)
        ch, b_ch = ar.alloc([8], F32)
        S.op("act", lambda e: e.activation(out=ee, in_=lam, func=AF.Exp, scale=-1.0), reads=[b_lam], writes=[b_ee])
        S.op("dve", lambda e: e.tensor_scalar(out=pp, in0=ee, scalar1=1.0 / 7, scalar2=-1.0 / 6, op0=ALU.mult, op1=ALU.add),
             reads=[b_ee], writes=[b_pp])
        for cst in (1.0 / 5, -1.0 / 4, 1.0 / 3, -1.0 / 2, 1.0):
            S.op("dve", lambda e: e.tensor_tensor(out=pp, in0=pp, in1=ee, op=ALU.mult), reads=[b_pp, b_ee], writes=[b_pp])
            S.op("dve", lambda e, cst=cst: e.tensor_scalar(out=pp, in0=pp, scalar1=cst, scalar2=None, op0=ALU.add),
                 reads=[b_pp], writes=[b_pp])
        S.op("dve", lambda e: e.scalar_tensor_tensor(out=ch, in0=pp, scalar=-4.0, in1=ee, op0=ALU.mult, op1=ALU.mult),
             reads=[b_pp, b_ee], writes=[b_ch])

        def bufs8():
            return [Buf() for _ in range(8)]

        xbs = [(ar.alloc([8, T], F32)[0], bufs8()) for _ in range(2)]
        xbbs = [(ar.alloc([8, T], BF16)[0], [Buf() for _ in range(4)]) for _ in range(2)]
        Rs = [(ar.alloc([8, T], F32)[0], bufs8()) for _ in range(2)]
        Is = [(ar.alloc([8, T], F32)[0], bufs8()) for _ in range(2)]
        Zs = [(ar.alloc([8, T], F32)[0], bufs8()) for _ in range(2)]
        HS = [ar.alloc([8, T], F32) for _ in range(2)]
        hnTs = [ar.alloc([8, T], BF16) for _ in range(3)]
        if bwd:
            st = self.norm_state()
            xts = [ar.alloc([NS, D], F32) for _ in range(2)]
            pres = [(ar.alloc([8, T + 3], F32)[0], bufs8(), Buf()) for _ in range(3)]
        else:
            ggs = [ar.alloc([8, T], F32) for _ in range(2)]
            gpre, b_gpre = ar.alloc([8, T], F32)
            HBs = [ar.alloc([8, T], F32) for _ in range(2)]
            yT, b_yT = ar.alloc([8, T], BF16)
            xrs = [ar.alloc([NS, D], F32) for _ in range(2)]
        hsb_v = self.hsb.rearrange("(c p) t -> p c t", p=128)
        xbd_v = self.xbd.rearrange("(c p) t -> p c t", p=128)
        xbbd_v = self.xbbd.rearrange("(c p) t -> p c t", p=128)
        order = list(range(NT - 1, -1, -1)) if bwd else list(range(NT))

        def row(tens, i):
            return tens[i * T:(i + 1) * T, :].rearrange("(s p) d -> p s d", p=128)

        def load_x(i):
            xt, b_xt = xts[i % 2]
            S.dma("sp", xt, row(self.x, i), writes=[b_xt])

        def early_b(i):
            xt, b_xt = xts[i % 2]
            hnT, b_hnT = hnTs[i % 3]
            pre, bp, b_halo = pres[i % 3]
            self.norm_T(xt, b_xt, g, b_g, hnT, b_hnT, st)
            S.dma("sp", self.hnT0d[:, :, i * T:(i + 1) * T], hnT, reads=[b_hnT], writes=[self.dbuf["hnT0"][i]])
            for m in range(8):
                bank = m % 2
                self.mm_proj(bank, wx, b_wx, m * 128, hnT, b_hnT)
                if m % 2 == 0:
                    S.op("act", lambda e, m=m, bank=bank: e.activation(out=pre[:, m, 2:T + 2], in_=ps[bank][:, 0:T], func=AF.Copy),
                         reads=[pb[bank]], writes=[bp[m]])
                else:
                    S.op("dve", lambda e, m=m, bank=bank: e.tensor_copy(out=pre[:, m, 2:T + 2], in_=ps[bank][:, 0:T]),
                         reads=[pb[bank]], writes=[bp[m]])

        def conv_start(i):
            pre, bp, b_halo = pres[i % 3]
            if i > 0:
                pl, bpl, _ = pres[(i - 1) % 3]
                S.op("pool", lambda e: e.tensor_copy(out=pre[:, :, 0:2], in_=pl[:, :, T:T + 2]), reads=bpl, writes=[b_halo])
            else:
                S.op("pool", lambda e: e.memset(pre[:, :, 0:2], 0.0), writes=[b_halo])
            if i < NT - 1:
                pr_, bpr, _ = pres[(i + 1) % 3]
                S.op("pool", lambda e: e.tensor_copy(out=pre[:, :, T + 2:T + 3], in_=pr_[:, :, 2:3]), reads=bpr, writes=[b_halo])
            else:
                S.op("pool", lambda e: e.memset(pre[:, :, T + 2:T + 3], 0.0), writes=[b_halo])

        def conv_tap0(i, m):
            pre, bp, b_halo = pres[i % 3]
            xb, bxb = xbs[i % 2]
            S.op("act", lambda e: e.activation(out=xb[:, m, :], in_=pre[:, m, 0:T], func=AF.Identity,
                                               scale=cw[0][0][:, m:m + 1], bias=cb[:, m:m + 1]),
                 reads=[bp[m], b_halo, cw[0][1], b_cb], writes=[bxb[m]])

        def conv_taps(i, m):
            pre, bp, b_halo = pres[i % 3]
            xb, bxb = xbs[i % 2]
            for k in range(1, 4):
                S.op("dve", lambda e, k=k: e.scalar_tensor_tensor(out=xb[:, m, :], in0=pre[:, m, k:k + T],
                                                                 scalar=cw[k][0][:, m:m + 1], in1=xb[:, m, :],
                                                                 op0=ALU.mult, op1=ALU.add),
                     reads=[bp[m], b_halo, cw[k][1], bxb[m]], writes=[bxb[m]])

        def conv_cast(i, n):
            xb, bxb = xbs[i % 2]
            xbb, bxbb = xbbs[i % 2]
            S.op("act", lambda e: e.activation(out=xbb[:, 2 * n:2 * n + 2, :], in_=xb[:, 2 * n:2 * n + 2, :], func=AF.Copy),
                 reads=bxb[2 * n:2 * n + 2], writes=[bxbb[n]])

        def conv_m(i, m):
            conv_tap0(i, m)
            conv_taps(i, m)
            if m % 2 == 1:
                conv_cast(i, m // 2)

        def early_na(i):
            xt, b_xt = xts[i % 2]
            self.norm_a(xt, b_xt, g, b_g, st)

        def early_nb(i):
            hnT, b_hnT = hnTs[i % 3]
            self.norm_b(hnT, b_hnT, st)
            S.dma("sp", self.hnT0d[:, :, i * T:(i + 1) * T], hnT, reads=[b_hnT], writes=[self.dbuf["hnT0"][i]])

        def early_proj(i, m):
            hnT, b_hnT = hnTs[i % 3]
            pre, bp, b_halo = pres[i % 3]
            bank = (0, 1, 4, 5)[m % 4]
            self.mm_proj(bank, wx, b_wx, m * 128, hnT, b_hnT)
            if m % 2 == 0:
                S.op("act", lambda e: e.activation(out=pre[:, m, 2:T + 2], in_=ps[bank][:, 0:T], func=AF.Copy),
                     reads=[pb[bank]], writes=[bp[m]])
            else:
                S.op("dve", lambda e: e.tensor_copy(out=pre[:, m, 2:T + 2], in_=ps[bank][:, 0:T]),
                     reads=[pb[bank]], writes=[bp[m]])

        def conv_end(i):
            xb, bxb = xbs[i % 2]
            xbb, bxbb = xbbs[i % 2]
            S.dma("sp", xbd_v[:, :, i * T:(i + 1) * T], xb, reads=bxb, writes=[self.dbuf["xb"][i]])
            S.dma("sp", xbbd_v[:, :, i * T:(i + 1) * T], xbb, reads=bxbb, writes=[self.dbuf["xbb"][i]])

        def load_f_early(i):
            hnT, b_hnT = hnTs[i % 3]
            xb, bxb = xbs[i % 2]
            xbb, bxbb = xbbs[i % 2]
            S.dma("sp", hnT, self.hnT0d[:, :, i * T:(i + 1) * T], reads=[self.dbuf["hnT0"][i]], writes=[b_hnT])
            S.dma("sp", xbb, xbbd_v[:, :, i * T:(i + 1) * T], reads=[self.dbuf["xbb"][i]], writes=bxbb)
            S.dma("sp", xb, xbd_v[:, :, i * T:(i + 1) * T], reads=[self.dbuf["xb"][i]], writes=bxb)

        def load_f_late(i):
            HB, b_HB = HBs[i % 2]
            xr, b_xr = xrs[i % 2]
            S.dma("sp", HB, hsb_v[:, :, i * T:(i + 1) * T], reads=[self.dbuf["hsb"][i]], writes=[b_HB])
            S.dma("sp", xr, row(self.x, i), writes=[b_xr])

        def gelu_m(i, m):
            hnT, b_hnT = hnTs[i % 3]
            bank = 4 + (m % 2)
            self.mm_proj(bank, wgb, b_wgb, m * 128, hnT, b_hnT)
            S.op("act", lambda e: e.activation(out=gpre[:, m, :], in_=ps[bank][:, 0:T], func=AF.Copy),
                 reads=[pb[bank]], writes=[b_gpre])

        def gelu_a(i):
            gg, b_gg = ggs[i % 2]
            S.op("act", lambda e: e.activation(out=gg, in_=gpre, func=AF.Square), reads=[b_gpre], writes=[b_gg])

        def gelu_b(i):
            gg, b_gg = ggs[i % 2]
            S.op("dve", lambda e: e.tensor_scalar(out=gg, in0=gg, scalar1=0.044715, scalar2=1.0, op0=ALU.mult, op1=ALU.add),
                 reads=[b_gg], writes=[b_gg])
            S.op("dve", lambda e: e.tensor_tensor(out=gg, in0=gg, in1=gpre, op=ALU.mult), reads=[b_gg, b_gpre], writes=[b_gg])
            S.op("act", lambda e: e.activation(out=gg, in_=gg, func=AF.Tanh, scale=0.7978845608028654), reads=[b_gg], writes=[b_gg])

        def gelu_c(i):
            gg, b_gg = ggs[i % 2]
            S.op("dve", lambda e: e.scalar_tensor_tensor(out=gg, in0=gg, scalar=1.0, in1=gpre, op0=ALU.add, op1=ALU.mult),
                 reads=[b_gg, b_gpre], writes=[b_gg])

        def gates(i):
            for m in range(8):
                gates_m(i, m)

        def gates_m(i, m):
            gates_pa(i, m)
            gates_dve(i, m)

        def gates_pa(i, m):
            xbb, bxbb = xbbs[i % 2]
            R, bR = Rs[i % 2]
            Ib, bI = Is[i % 2]
            n, hm = m // 2, m % 2
            br, bi = 2, 3
            for kc in range(2):
                S.op("pe", lambda e, kc=kc: e.matmul(ps[br][:, 0:T], lhsT=ga[:, n, kc, hm * 128:(hm + 1) * 128],
                                                    rhs=xbb[:, 2 * n + kc, :], start=(kc == 0), stop=(kc == 1)),
                     reads=[b_ga, bxbb[n]], writes=[pb[br]])
            for kc in range(2):
                S.op("pe", lambda e, kc=kc: e.matmul(ps[bi][:, 0:T], lhsT=gx[:, n, kc, hm * 128:(hm + 1) * 128],
                                                    rhs=xbb[:, 2 * n + kc, :], start=(kc == 0), stop=(kc == 1)),
                     reads=[b_gx, bxbb[n]], writes=[pb[bi]])
            S.op("act", lambda e: e.activation(out=R[:, m, :], in_=ps[br][:, 0:T], func=AF.Tanh, scale=0.5, bias=hba[:, m:m + 1]),
                 reads=[pb[br], b_hba], writes=[bR[m]])
            S.op("act", lambda e: e.activation(out=Ib[:, m, :], in_=ps[bi][:, 0:T], func=AF.Tanh, scale=0.5, bias=hbx[:, m:m + 1]),
                 reads=[pb[bi], b_hbx], writes=[bI[m]])
            S.op("act", lambda e: e.activation(out=R[:, m, :], in_=R[:, m, :], func=AF.Exp, scale=ch[:, m:m + 1], bias=ch[:, m:m + 1]),
                 reads=[bR[m], b_ch], writes=[bR[m]])
            if not bwd:
                Zf, bZf = Zs[i % 2]
                S.op("act", lambda e: e.activation(out=Zf[:, m, :], in_=R[:, m, :], func=AF.Square), reads=[bR[m]], writes=[bZf[m]])

        def gates_dve(i, m):
            xb, bxb = xbs[i % 2]
            R, bR = Rs[i % 2]
            Ib, bI = Is[i % 2]
            Z, bZ = Zs[i % 2]
            S.op("dve", lambda e: e.scalar_tensor_tensor(out=Ib[:, m, :], in0=Ib[:, m, :], scalar=1.0, in1=xb[:, m, :],
                                                         op0=ALU.add, op1=ALU.mult),
                 reads=[bI[m], bxb[m]], writes=[bI[m]])
            if bwd:
                S.op("dve", lambda e: e.scalar_tensor_tensor(out=Z[:, m, :], in0=R[:, m, :], scalar=-1.0, in1=R[:, m, :],
                                                             op0=ALU.mult, op1=ALU.mult),
                     reads=[bR[m]], writes=[bZ[m]])

        def scan(i, prev_hs):
            scan_a(i)
            scan_a2(i)
            return scan_b(i, prev_hs)

        def scan_a(i):
            Z, bZ = Zs[i % 2]
            S.op("act", lambda e: e.activation(out=Z, in_=Z, func=AF.Sqrt, scale=(1.0 if bwd else -1.0), bias=1.0), reads=bZ, writes=bZ)

        def scan_a2(i):
            Ib, bI = Is[i % 2]
            Z, bZ = Zs[i % 2]
            S.op("dve", lambda e: e.scalar_tensor_tensor(out=Z, in0=Z, scalar=0.5, in1=Ib, op0=ALU.mult, op1=ALU.mult),
                 reads=bZ + bI, writes=bZ)

        def scan_b(i, prev_hs):
            for m in range(8):
                scan_m(i, prev_hs, m)
            return HS[i % 2]

        def scan_m(i, prev_hs, m):
            R, bR = Rs[i % 2]
            Z, bZ = Zs[i % 2]
            hs, b_hs = HS[i % 2]
            if True:
                if prev_hs is None:
                    init, rd = 0.0, []
                else:
                    ph, b_ph = prev_hs
                    init = ph[:, m, 0:1] if bwd else ph[:, m, T - 1:T]
                    rd = [b_ph]
                if bwd:
                    o_, a_, z_ = hs[:, m, ::-1], R[:, m, ::-1], Z[:, m, ::-1]
                else:
                    o_, a_, z_ = hs[:, m, :], R[:, m, :], Z[:, m, :]
                S.op("dve", lambda e, o_=o_, a_=a_, z_=z_, init=init: e.tensor_tensor_scan(out=o_, data0=a_, data1=z_,
                                                                                          initial=init, op0=ALU.mult,
                                                                                          op1=ALU.add),
                     reads=[bR[m], bZ[m]] + rd, writes=[b_hs])
            return (hs, b_hs)

        def finish_f(i, hsb_):
            hs, b_hs = hsb_
            HB, b_HB = HBs[i % 2]
            xr, b_xr = xrs[i % 2]
            gg, b_gg = ggs[i % 2]
            S.op("dve", lambda e: e.tensor_tensor(out=HB, in0=HB, in1=hs, op=ALU.add), reads=[b_HB, b_hs], writes=[b_HB])
            S.op("dve", lambda e: e.scalar_tensor_tensor(out=yT, in0=HB, scalar=0.5, in1=gg, op0=ALU.mult, op1=ALU.mult),
                 reads=[b_HB, b_gg], writes=[b_yT])
            for s in range(NS):
                for n in range(2):
                    bank = 4 + s * 2 + n
                    self.mm_tok(bank, yT, b_yT, s, wo, b_wo, n * 512, 8)

        def finish_b(i):
            xr, b_xr = xrs[i % 2]
            for s in range(NS):
                for n in range(2):
                    bank = 4 + s * 2 + n
                    xs = xr[:, s, n * 512:(n + 1) * 512]
                    S.op("dve", lambda e, xs=xs, bank=bank: e.tensor_tensor(out=xs, in0=xs, in1=ps[bank][:, 0:512], op=ALU.add),
                         reads=[pb[bank]], writes=[b_xr])
            S.dma("sp", row(self.hm0, i), xr, reads=[b_xr], writes=[self.dbuf["hm0"][i]])

        n_ = len(order)
        prev_hs = None
        if bwd:
            def E(k):
                early_b(order[k])
                if k + 2 < n_:
                    load_x(order[k + 2])

            def SC(k):
                j = order[k]
                hsb_ = scan(j, prev_hs_box[0])
                prev_hs_box[0] = hsb_
                S.dma("sp", hsb_v[:, :, j * T:(j + 1) * T], hsb_[0], reads=[hsb_[1]], writes=[self.dbuf["hsb"][j]])

            prev_hs_box = [None]
            load_x(order[0])
            load_x(order[1])
            E(0)
            E(1)
            E(2)
            conv_start(order[0])
            for m in range(8):
                conv_m(order[0], m)
            conv_end(order[0])
            for t in range(n_):
                gi = order[t]
                cj = order[t + 1] if t + 1 < n_ else None
                sj = order[t - 1] if t >= 1 else None
                ej = order[t + 3] if t + 3 < n_ else None
                if cj is not None:
                    conv_start(cj)
                if ej is not None:
                    early_na(ej)
                if sj is not None:
                    scan_a(sj)
                for k in range(10):
                    if cj is not None and k < 8:
                        conv_tap0(cj, k)
                    if cj is not None and k >= 3 and k % 2 == 1:
                        conv_cast(cj, (k - 3) // 2)
                    if k < 8:
                        gates_pa(gi, k)
                    if ej is not None:
                        if k == 1:
                            early_nb(ej)
                        elif k >= 2:
                            early_proj(ej, k - 2)
                    if cj is not None and 1 <= k <= 8:
                        conv_taps(cj, k - 1)
                    if 1 <= k <= 8:
                        gates_dve(gi, k - 1)
                    if sj is not None:
                        if k == 0:
                            scan_a2(sj)
                        elif k <= 8:
                            scan_m(sj, prev_hs_box[0], k - 1)
                if cj is not None:
                    conv_end(cj)
                if sj is not None:
                    hsb_ = HS[sj % 2]
                    prev_hs_box[0] = hsb_
                    S.dma("sp", hsb_v[:, :, sj * T:(sj + 1) * T], hsb_[0], reads=[hsb_[1]], writes=[self.dbuf["hsb"][sj]])
                if t + 5 < n_:
                    load_x(order[t + 5])
            SC(n_ - 1)
        else:
            load_f_early(0)
            load_f_late(0)
            load_f_early(1)
            for idx in range(n_):
                i = order[idx]
                j = order[idx - 1] if idx >= 1 else None
                if j is not None:
                    scan_a(j)
                for k in range(10):
                    if k < 8:
                        gelu_m(i, k)
                        gates_pa(i, k)
                    if 1 <= k <= 8:
                        gates_dve(i, k - 1)
                    if j is not None:
                        if k == 0:
                            scan_a2(j)
                        elif k <= 8:
                            scan_m(j, prev_hs, k - 1)
                if j is not None:
                    prev_hs = HS[j % 2]
                    finish_f(j, prev_hs)
                gelu_a(i)
                gelu_b(i)
                gelu_c(i)
                if j is not None:
                    finish_b(j)
                if i + 1 < NT:
                    load_f_late(i + 1)
                if i + 2 < NT:
                    load_f_early(i + 2)
            j = order[-1]
            prev_hs = scan(j, prev_hs)
            finish_f(j, prev_hs)
            finish_b(j)

    def rotary(self, pre, b_pre, outT, b_out, c0, t0, cosT, sinT, b_cs, tmp, nh=1):
        S = self.S
        (ta, b_ta), (tb, b_tb), (tc, b_tc), (td, b_td) = tmp
        x1, x2 = pre[:, c0:c0 + 2 * nh:2, :], pre[:, c0 + 1:c0 + 2 * nh:2, :]
        o1, o2 = outT[:, c0:c0 + 2 * nh:2, :], outT[:, c0 + 1:c0 + 2 * nh:2, :]
        co = cosT[:, t0:t0 + T].unsqueeze(1).broadcast_to([128, nh, T])
        si = sinT[:, t0:t0 + T].unsqueeze(1).broadcast_to([128, nh, T])
        S.op("dve", lambda e: e.tensor_tensor(out=ta, in0=x1, in1=co, op=ALU.mult), reads=[b_pre, b_cs], writes=[b_ta])
        S.op("dve", lambda e: e.tensor_tensor(out=tb, in0=x2, in1=si, op=ALU.mult), reads=[b_pre, b_cs], writes=[b_tb])
        S.op("dve", lambda e: e.tensor_tensor(out=tc, in0=x2, in1=co, op=ALU.mult), reads=[b_pre, b_cs], writes=[b_tc])
        S.op("dve", lambda e: e.tensor_tensor(out=td, in0=x1, in1=si, op=ALU.mult), reads=[b_pre, b_cs], writes=[b_td])
        S.op("dve", lambda e: e.tensor_tensor(out=o1, in0=ta, in1=tb, op=ALU.subtract), reads=[b_ta, b_tb], writes=[b_out])
        S.op("dve", lambda e: e.tensor_tensor(out=o2, in0=tc, in1=td, op=ALU.add), reads=[b_tc, b_td], writes=[b_out])

    def load_tables(self):
        ar, S = self.ar, self.S
        cosT, b_cs = ar.alloc([SEQ], F32)
        sinT, _ = ar.alloc([SEQ], F32)
        S.dma("sp", cosT, self.cosT_d, writes=[b_cs])
        S.dma("sp", sinT, self.sinT_d, writes=[b_cs])
        zz, b_zz = ar.alloc([8], F32)
        S.dma("sp", zz, self.zz_d, writes=[b_zz])
        return cosT, sinT, b_cs, zz, b_zz

    def phase_ret_bwd(self):
        S, ar = self.S, self.ar
        S.barrier()
        ar.reset()
        gf, gb = _gammas()
        ps, pb = self.ps, self.pb
        wk, b_wk = ar.alloc([8, D], BF16)
        self.load_w(wk, b_wk, self.ret_w_in, 8, D, D, 256)
        wv, b_wv = ar.alloc([8, 2 * D], BF16)
        self.load_w(wv, b_wv, self.ret_w_in, 8, 2 * D, 2 * D, 512)
        g, b_g = self.load_gain(self.ln_mix[1])
        zz, b_zz = ar.alloc([8], F32)
        S.dma("sp", zz, self.zz_d, writes=[b_zz])
        csts = [(ar.alloc([T], F32)[0], ar.alloc([T], F32)[0], Buf()) for _ in range(3)]
        st = self.norm_state()
        xts = [ar.alloc([NS, D], F32) for _ in range(3)]
        hnTs = [ar.alloc([8, T], BF16) for _ in range(2)]
        kpres = [ar.alloc([8, T], F32) for _ in range(2)]
        kTs = [ar.alloc([8, T], BF16) for _ in range(2)]
        vs = [ar.alloc([NS, 2 * D], BF16) for _ in range(2)]
        Sbs = [ar.alloc([8, 512], F32) for _ in range(2)]
        cc_box = [0]
        SbB = [ar.alloc([8, 512], BF16) for _ in range(2)]
        kbs = [ar.alloc([D], BF16) for _ in range(2)]
        tmp = [ar.alloc([4, T], F32) for _ in range(4)]
        S.op("pool", lambda e: e.memset(Sbs[0][0], 0.0), writes=[Sbs[0][1]])

        def row(tens, i):
            return tens[i * T:(i + 1) * T, :].rearrange("(s p) d -> p s d", p=128)

        def load_x(i):
            S.dma("sp", xts[i % 3][0], row(self.h1, i), reads=[self.dbuf["h1"][i]], writes=[xts[i % 3][1]])
            co, si, b_c = csts[i % 3]
            S.dma("sp", co, self.cosT_d[:, i * T:(i + 1) * T], writes=[b_c])
            S.dma("sp", si, self.sinT_d[:, i * T:(i + 1) * T], writes=[b_c])

        def na(i):
            self.norm_a(xts[i % 3][0], xts[i % 3][1], g, b_g, st)

        def nb(i):
            hnT, b_hnT = hnTs[i % 2]
            self.norm_b(hnT, b_hnT, st)
            S.dma("sp", self.hnT1d[:, :, i * T:(i + 1) * T], hnT, reads=[b_hnT], writes=[self.dbuf["hnT1"][i]])

        def PK(i):
            hnT, b_hnT = hnTs[i % 2]
            kpre, b_kpre = kpres[i % 2]
            kT, b_kT = kTs[i % 2]
            for m in range(8):
                bank = m % 2
                self.mm_proj(bank, wk, b_wk, m * 128, hnT, b_hnT)
                S.op("act", lambda e, m=m, bank=bank: e.activation(out=kpre[:, m, :], in_=ps[bank][:, 0:T], func=AF.Copy),
                     reads=[pb[bank]], writes=[b_kpre])
            co, si, b_c = csts[i % 3]
            self.rotary(kpre, b_kpre, kT, b_kT, 0, 0, co, si, b_c, tmp, nh=4)
            S.dma("sp", self.kTd[:, :, i * T:(i + 1) * T], kT, reads=[b_kT], writes=[self.dbuf["kT"][i]])

        def PV(i, s):
            hnT, b_hnT = hnTs[i % 2]
            v, b_v = vs[i % 2]
            for n in range(4):
                bank = n % 2
                self.mm_tok(bank, hnT, b_hnT, s, wv, b_wv, n * 512, 8)
                S.op("act", lambda e, n=n, bank=bank: e.activation(out=v[:, s, n * 512:(n + 1) * 512], in_=ps[bank][:, 0:512], func=AF.Copy),
                     reads=[pb[bank]], writes=[b_v])

        def store_v(i):
            v, b_v = vs[i % 2]
            S.dma("sp", self.vd[i * T:(i + 1) * T, :].rearrange("(s p) e -> p s e", p=128), v, reads=[b_v], writes=[self.dbuf["v"][i]])

        def CH(i, s):
            c = i * NS + s
            kT, b_kT = kTs[i % 2]
            v, b_v = vs[i % 2]
            sbb, b_sbb = SbB[c % 2]
            kb, b_kb = kbs[c % 2]
            Sb, b_Sb = Sbs[cc_box[0] % 2]
            Sn, b_Sn = Sbs[(cc_box[0] + 1) % 2]
            cc_box[0] += 1
            S.op("act", lambda e: e.activation(out=sbb, in_=Sb, func=AF.Copy), reads=[b_Sb], writes=[b_sbb])
            S.dma("sp", self.sbst[c], sbb, reads=[b_sbb], writes=[self.dsb[c]])
            bank = 6 + (c % 2)
            psb = ps[bank][:].bitcast(BF16)
            for m in range(8):
                S.op("pe", lambda e, m=m: e.transpose(out=psb[:, m * 128:(m + 1) * 128], in_=kT[:, m, s * 128:(s + 1) * 128],
                                                      identity=self.ident), reads=[b_kT, self.b_ident], writes=[pb[bank]])
            for h in range(4):
                S.op("dve", lambda e, h=h: e.tensor_scalar(out=kb[:, h * 256:(h + 1) * 256], in0=psb[:, h * 256:(h + 1) * 256],
                                                          scalar1=zz[:, 4 + h:5 + h], scalar2=None, op0=ALU.mult),
                     reads=[pb[bank], b_zz], writes=[b_kb])
            for h in range(4):
                g128 = float(gb[h] ** 128)
                for dc in range(2):
                    bk = 2 + ((h * 2 + dc) % 4)
                    S.op("pe", lambda e, h=h, dc=dc, bk=bk: e.matmul(ps[bk][:, 0:512], lhsT=kb[:, h * 256 + dc * 128:h * 256 + (dc + 1) * 128],
                                                                    rhs=v[:, s, h * 512:(h + 1) * 512], start=True, stop=True),
                         reads=[b_kb, b_v], writes=[pb[bk]])
                    S.op("dve", lambda e, h=h, dc=dc, bk=bk, g128=g128: e.scalar_tensor_tensor(
                        out=Sn[:, h * 2 + dc, :], in0=Sb[:, h * 2 + dc, :], scalar=g128, in1=ps[bk][:, 0:512],
                        op0=ALU.mult, op1=ALU.add), reads=[pb[bk], b_Sb], writes=[b_Sn])

        assert NS == 2
        order = list(range(NT - 1, -1, -1))
        n_ = len(order)
        load_x(order[0])
        load_x(order[1])
        load_x(order[2])
        na(order[0])
        nb(order[0])
        PK(order[0])
        PV(order[0], 1)
        PV(order[0], 0)
        store_v(order[0])
        na(order[1])
        for idx in range(n_):
            i = order[idx]
            nxt = order[idx + 1] if idx + 1 < n_ else None
            nxt2 = order[idx + 2] if idx + 2 < n_ else None
            CH(i, 1)
            if nxt is not None:
                nb(nxt)
                PK(nxt)
            if nxt2 is not None:
                na(nxt2)
            if idx + 3 < n_:
                load_x(order[idx + 3])
            if nxt is not None:
                PV(nxt, 1)
                PV(nxt, 0)
                store_v(nxt)
            CH(i, 0)

    def phase_ret_fwd_all(self):
        S, ar = self.S, self.ar
        S.barrier()
        ar.reset()
        gf, gb = _gammas()
        ps, pb = self.ps, self.pb
        cosT, sinT, b_cs, zz, b_zz = self.load_tables()
        W = []
        for par in range(2):
            wq, b_wq = ar.alloc([8, 256], BF16)
            wgt, b_wgt = ar.alloc([8, 512], BF16)
            wo, b_wo = ar.alloc([4, D], BF16)
            MT, b_tabs = ar.alloc([128], F32)
            XF2, _ = ar.alloc([T], F32)
            XB2, _ = ar.alloc([T], F32)
            W.append((wq, b_wq, wgt, b_wgt, wo, b_wo, MT, b_tabs, XF2, XB2))

        def load_head(h):
            wq, b_wq, wgt, b_wgt, wo, b_wo, MT, b_tabs, XF2, XB2 = W[h % 2]
            self.load_w(wq, b_wq, self.ret_w_in, 8, h * 256, 256, 256)
            self.load_w(wgt, b_wgt, self.ret_w_in, 8, 4 * D + h * 512, 512, 512)
            S.dma("pool", wo, self.ret_w_out[h * 512:(h + 1) * 512, :].rearrange("(k p) n -> p k n", p=128), writes=[b_wo])
            S.dma("sp", MT, self.tabs_d[:, h, :], writes=[b_tabs])
            for s in range(NS):
                S.dma("sp", XF2[:, s * 128:(s + 1) * 128], self.tabs_d[:, 4 + h, :], writes=[b_tabs])
                S.dma("sp", XB2[:, s * 128:(s + 1) * 128], self.tabs_d[:, 8 + h, :], writes=[b_tabs])

        hacc_s = [ar.alloc([NS, D], F32) for _ in range(3)]
        hnTs = [ar.alloc([8, T], BF16) for _ in range(3)]
        qps = [ar.alloc([2, T], F32) for _ in range(2)]
        qTs = [ar.alloc([2, T], BF16) for _ in range(2)]
        kTs = [ar.alloc([2, T], BF16) for _ in range(2)]
        vs = [ar.alloc([NS, 512], BF16) for _ in range(2)]
        sgs = [ar.alloc([NS, 512], F32) for _ in range(2)]
        SMs = [ar.alloc([NS, 128], BF16) for _ in range(2)]
        kfs = [ar.alloc([NS, 256], BF16) for _ in range(2)]
        qfs = [ar.alloc([2, T], BF16) for _ in range(2)]
        qbs = [ar.alloc([2, T], BF16) for _ in range(2)]
        tgs = [ar.alloc([512], F32) for _ in range(2)]
        Sf, b_Sf = ar.alloc([2, 512], F32)
        SfB = [ar.alloc([2, 512], BF16) for _ in range(2)]
        SbL = [ar.alloc([2, 512], BF16) for _ in range(4)]
        zs = [ar.alloc([512], BF16) for _ in range(2)]
        zTs = [ar.alloc([4, 128], BF16) for _ in range(2)]
        rss = [ar.alloc([8], F32) for _ in range(2)]
        junk2, b_junk2 = ar.alloc([512], BF16)
        tmp = [ar.alloc([1, T], F32) for _ in range(4)]
        SfZ, b_SfZ = ar.alloc([2, 512], BF16)
        S.op("pool", lambda e: e.memset(SfZ, 0.0), writes=[b_SfZ])
        NG = 4 * NT

        def row(tens, i):
            return tens[i * T:(i + 1) * T, :].rearrange("(s p) d -> p s d", p=128)

        def L(gt):
            h, i = divmod(gt, NT)
            hnT, b_hnT = hnTs[gt % 3]
            kT, b_kT = kTs[gt % 2]
            v, b_v = vs[gt % 2]
            ha, b_ha = hacc_s[gt % 3]
            S.dma("sp", hnT, self.hnT1d[:, :, i * T:(i + 1) * T], reads=[self.dbuf["hnT1"][i]], writes=[b_hnT])
            S.dma("sp", kT, self.kTd[:, 2 * h:2 * h + 2, i * T:(i + 1) * T], reads=[self.dbuf["kT"][i]], writes=[b_kT])
            S.dma("sp", v, self.vd[i * T:(i + 1) * T, h * 512:(h + 1) * 512].rearrange("(s p) e -> p s e", p=128),
                  reads=[self.dbuf["v"][i]], writes=[b_v])
            if h > 0:
                S.dma("sp", ha, row(self.hm1, i), reads=[self.dbuf["hm1"][i]], writes=[b_ha])
            else:
                S.dma("sp", ha, row(self.h1, i), reads=[self.dbuf["h1"][i]], writes=[b_ha])

        def load_sb(gc):
            h, c = divmod(gc, NCH)
            sbl, b_sbl = SbL[gc % 4]
            S.dma("sp", sbl, self.sbst[c][:, 2 * h:2 * h + 2, :], reads=[self.dsb[c]], writes=[b_sbl])

        def P1(gt):
            h, i = divmod(gt, NT)
            wq, b_wq, wgt, b_wgt, wo, b_wo, MT, b_tabs, XF2, XB2 = W[h % 2]
            hnT, b_hnT = hnTs[gt % 3]
            qp, b_qp = qps[gt % 2]
            qT, b_qT = qTs[gt % 2]
            qf, b_qf = qfs[gt % 2]
            qb, b_qb = qbs[gt % 2]
            for dc in range(2):
                for kc in range(8):
                    S.op("pe", lambda e, kc=kc, dc=dc: e.matmul(ps[dc][:, 0:T], lhsT=wq[:, kc, dc * 128:(dc + 1) * 128],
                                                               rhs=hnT[:, kc, :], start=(kc == 0), stop=(kc == 7)),
                         reads=[b_wq, b_hnT], writes=[pb[dc]])
                S.op("act", lambda e, dc=dc: e.activation(out=qp[:, dc, :], in_=ps[dc][:, 0:T], func=AF.Copy),
                     reads=[pb[dc]], writes=[b_qp])
            self.rotary(qp, b_qp, qT, b_qT, 0, i * T, cosT, sinT, b_cs, tmp)
            for dc in range(2):
                S.op("dve", lambda e, dc=dc: e.tensor_tensor(out=qf[:, dc, :], in0=qT[:, dc, :], in1=XF2, op=ALU.mult),
                     reads=[b_qT, b_tabs], writes=[b_qf])
                S.op("dve", lambda e, dc=dc: e.tensor_tensor(out=qb[:, dc, :], in0=qT[:, dc, :], in1=XB2, op=ALU.mult),
                     reads=[b_qT, b_tabs], writes=[b_qb])

        def P2(gt, s):
            h, i = divmod(gt, NT)
            wq, b_wq, wgt, b_wgt, wo, b_wo, MT, b_tabs, XF2, XB2 = W[h % 2]
            hnT, b_hnT = hnTs[gt % 3]
            sgate, b_sgate = sgs[gt % 2]
            tg, b_tg = tgs[s % 2]
            for k in range(8):
                S.op("pe", lambda e, k=k: e.matmul(ps[0][:, 0:512], lhsT=hnT[:, k, s * 128:(s + 1) * 128], rhs=wgt[:, k, :],
                                                  start=(k == 0), stop=(k == 7)), reads=[b_hnT, b_wgt], writes=[pb[0]])
            S.op("act", lambda e: e.activation(out=tg, in_=ps[0][:, 0:512], func=AF.Tanh, scale=0.5), reads=[pb[0]], writes=[b_tg])
            S.op("dve", lambda e: e.scalar_tensor_tensor(out=sgate[:, s, :], in0=tg, scalar=1.0, in1=ps[0][:, 0:512],
                                                         op0=ALU.add, op1=ALU.mult),
                 reads=[b_tg, pb[0]], writes=[b_sgate])

        def P3(gt):
            h, i = divmod(gt, NT)
            wq, b_wq, wgt, b_wgt, wo, b_wo, MT, b_tabs, XF2, XB2 = W[h % 2]
            qT, b_qT = qTs[gt % 2]
            kT, b_kT = kTs[gt % 2]
            SM, b_SM = SMs[gt % 2]
            kf, b_kf = kfs[gt % 2]
            psb6 = ps[6][:].bitcast(BF16)
            for s in range(NS):
                cs = slice(s * 128, (s + 1) * 128)
                for dc in range(2):
                    S.op("pe", lambda e, dc=dc, cs=cs, s=s: e.transpose(out=psb6[:, s * 256 + dc * 128:s * 256 + (dc + 1) * 128],
                                                                       in_=kT[:, dc, cs], identity=self.ident),
                         reads=[b_kT, self.b_ident], writes=[pb[6]])
            for s in range(NS):
                cs = slice(s * 128, (s + 1) * 128)
                for dc in range(2):
                    S.op("pe", lambda e, dc=dc, cs=cs, s=s: e.matmul(ps[2][:, s * 128:(s + 1) * 128], lhsT=kT[:, dc, cs], rhs=qT[:, dc, cs],
                                                                    start=(dc == 0), stop=(dc == 1)), reads=[b_kT, b_qT], writes=[pb[2]])
            S.op("dve", lambda e: e.tensor_scalar(out=kf, in0=psb6[:, 0:NS * 256].rearrange("p (s d) -> p s d", s=NS), scalar1=zz[:, h:h + 1],
                                                  scalar2=None, op0=ALU.mult), reads=[pb[6], b_zz], writes=[b_kf])
            for s in range(NS):
                S.op("dve", lambda e, s=s: e.tensor_tensor(out=SM[:, s, :], in0=ps[2][:, s * 128:(s + 1) * 128], in1=MT, op=ALU.mult),
                     reads=[pb[2], b_tabs], writes=[b_SM])

        def C(gt, s):
            h, i = divmod(gt, NT)
            g128 = float(gf[h] ** 128)
            c = i * NS + s
            gc = h * NCH + c
            cp = gc % 2
            cs = slice(s * 128, (s + 1) * 128)
            v, b_v = vs[gt % 2]
            sgate, b_sgate = sgs[gt % 2]
            SM, b_SM = SMs[gt % 2]
            kf, b_kf = kfs[gt % 2]
            qf, b_qf = qfs[gt % 2]
            qb, b_qb = qbs[gt % 2]
            sfb, b_sfb = (SfZ, b_SfZ) if c == 0 else SfB[cp]
            sfb_n, b_sfb_n = SfB[1 - cp]
            sbl, b_sbl = SbL[gc % 4]
            z, b_z = zs[cp]
            rs, b_rs = rss[cp]
            yb = 3 + cp
            for dc in range(2):
                bank = 5 if dc == 0 else 1
                S.op("pe", lambda e, dc=dc, bank=bank: e.matmul(ps[bank][:, 0:512], lhsT=kf[:, s, dc * 128:(dc + 1) * 128],
                                                               rhs=v[:, s, :], start=True, stop=True),
                     reads=[b_kf, b_v], writes=[pb[bank]])
                if c == 0:
                    S.op("dve", lambda e, dc=dc, bank=bank: e.tensor_copy(out=Sf[:, dc, :], in_=ps[bank][:, 0:512]),
                         reads=[pb[bank]], writes=[b_Sf])
                else:
                    S.op("dve", lambda e, dc=dc, bank=bank: e.scalar_tensor_tensor(out=Sf[:, dc, :], in0=Sf[:, dc, :], scalar=g128,
                                                                                  in1=ps[bank][:, 0:512], op0=ALU.mult, op1=ALU.add),
                         reads=[pb[bank]], writes=[b_Sf])
            S.op("pe", lambda e: e.matmul(ps[yb][:, 0:512], lhsT=SM[:, s, :], rhs=v[:, s, :], start=True, stop=False),
                 reads=[b_SM, b_v], writes=[pb[yb]])
            for dc in range(2):
                S.op("pe", lambda e, dc=dc: e.matmul(ps[yb][:, 0:512], lhsT=qf[:, dc, cs], rhs=sfb[:, dc, :], start=False, stop=False),
                     reads=[b_qf, b_sfb], writes=[pb[yb]])
            for dc in range(2):
                S.op("pe", lambda e, dc=dc: e.matmul(ps[yb][:, 0:512], lhsT=qb[:, dc, cs], rhs=sbl[:, dc, :], start=False, stop=(dc == 1)),
                     reads=[b_qb, b_sbl], writes=[pb[yb]])
            S.op("act", lambda e: e.activation(out=sfb_n, in_=Sf, func=AF.Copy), reads=[b_Sf], writes=[b_sfb_n])
            S.op("dve", lambda e: e.tensor_tensor(out=z, in0=ps[yb][:, 0:512], in1=sgate[:, s, :], op=ALU.mult),
                 reads=[pb[yb], b_sgate], writes=[b_z])
            S.op("act", lambda e: e.activation(out=junk2, in_=ps[yb][:, 0:512], func=AF.Square, accum_out=rs[:, 0:1]),
                 reads=[pb[yb], b_z], writes=[b_junk2, b_rs])
            self.rstd_from(rs, b_rs, 1, 4.0 / 512, 4.0 * EPS)
            if gc + 2 < 4 * NCH:
                load_sb(gc + 2)

        def S2a(gt, s):
            h, i = divmod(gt, NT)
            gc = h * NCH + i * NS + s
            cp = gc % 2
            z, b_z = zs[cp]
            zT, b_zT = zTs[cp]
            psb7 = ps[7][:].bitcast(BF16)
            for ec in range(4):
                S.op("pe", lambda e, ec=ec: e.transpose(out=psb7[:, ec * 128:(ec + 1) * 128], in_=z[:, ec * 128:(ec + 1) * 128],
                                                        identity=self.ident), reads=[b_z, self.b_ident], writes=[pb[7]])
            S.op("act", lambda e: e.activation(out=zT, in_=psb7[:, 0:512].rearrange("p (k t) -> p k t", k=4), func=AF.Copy),
                 reads=[pb[7]], writes=[b_zT])

        def S2b(gt, s):
            h, i = divmod(gt, NT)
            wq, b_wq, wgt, b_wgt, wo, b_wo, MT, b_tabs, XF2, XB2 = W[h % 2]
            gc = h * NCH + i * NS + s
            cp = gc % 2
            zT, b_zT = zTs[cp]
            rs, b_rs = rss[cp]
            hacc, b_hacc = hacc_s[gt % 3]
            for n in range(2):
                bank = 7 if n == 0 else 3 + cp
                for ec in range(4):
                    S.op("pe", lambda e, ec=ec, n=n, bank=bank: e.matmul(ps[bank][:, 0:512], lhsT=zT[:, ec, :],
                                                                         rhs=wo[:, ec, n * 512:(n + 1) * 512],
                                                                         start=(ec == 0), stop=(ec == 3)),
                         reads=[b_zT, b_wo], writes=[pb[bank]])
                hs_ = hacc[:, s, n * 512:(n + 1) * 512]
                S.op("dve", lambda e, hs_=hs_, bank=bank: e.scalar_tensor_tensor(out=hs_, in0=ps[bank][:, 0:512], scalar=rs[:, 0:1],
                                                                                in1=hs_, op0=ALU.mult, op1=ALU.add),
                     reads=[pb[bank], b_rs], writes=[b_hacc])
            if s == NS - 1:
                S.dma("sp", row(self.hm1, i), hacc, reads=[b_hacc], writes=[self.dbuf["hm1"][i]])

        assert NS == 2
        load_head(0)
        load_head(1)
        L(0)
        L(1)
        load_sb(0)
        load_sb(1)
        P1(0)
        P2(0, 0)
        P2(0, 1)
        for gt in range(NG):
            h, i = divmod(gt, NT)
            nxt = gt + 1 < NG
            P3(gt)
            if gt > 0:
                S2b(gt - 1, 1)
            if i == 0 and 1 <= h and h + 1 < 4:
                load_head(h + 1)
            C(gt, 0)
            if nxt:
                P1(gt + 1)
            C(gt, 1)
            S2a(gt, 0)
            if nxt:
                P2(gt + 1, 0)
            S2b(gt, 0)
            if nxt:
                P2(gt + 1, 1)
            S2a(gt, 1)
            if gt + 2 < NG:
                L(gt + 2)
        S2b(NG - 1, 1)

    def build(self):
        ph = self.phases
        on = lambda p: ph is None or p in ph
        if on("A1"):
            self.phase_lru(True)
        if on("A2"):
            self.phase_lru(False)
        if on("F0"):
            self.phase_ffn(0, self.hm0, "hm0", self.h1, "h1", False)
        if on("R1"):
            self.phase_ret_bwd()
        if on("R20") or on("R23"):
            self.phase_ret_fwd_all()
        if on("F1"):
            self.phase_ffn(1, self.hm1, "hm1", self.out, "out", True)
        self.S.emit()
        return self.nc


_CONST = None


def make_in_maps(inputs, n=NCORES):
    global _CONST
    if _CONST is None:
        _CONST = host_tables()
    cosT, sinT, tabs, zz = _CONST
    f = lambda a: np.ascontiguousarray(np.asarray(a, dtype=np.float32))
    shared = {
        "ln_mix": f(inputs["ln_mix"]), "ln_ffn": f(inputs["ln_ffn"]), "ln_final": f(inputs["ln_final"]),
        "lru_w_in": f(inputs["lru_w_in"][0]), "lru_conv_w": f(inputs["lru_conv_w"][0]),
        "lru_conv_b": f(inputs["lru_conv_b"][0]), "lru_gate_a_w": f(inputs["lru_gate_a_w"][0]),
        "lru_gate_a_b": f(inputs["lru_gate_a_b"][0]), "lru_gate_x_w": f(inputs["lru_gate_x_w"][0]),
        "lru_gate_x_b": f(inputs["lru_gate_x_b"][0]), "lru_lambda": f(inputs["lru_lambda"][0]),
        "lru_w_out": f(inputs["lru_w_out"][0]), "ret_w_in": f(inputs["ret_w_in"][0]),
        "ret_w_out": f(inputs["ret_w_out"][0]), "ffn_w_gate": f(inputs["ffn_w_gate"]),
        "ffn_w_up": f(inputs["ffn_w_up"]), "ffn_w_down": f(inputs["ffn_w_down"]),
        "cosT": cosT, "sinT": sinT, "tabs": tabs, "zz": zz,
    }
    x = f(inputs["x"])
    return [dict(shared, x=x[b]) for b in range(n)]


def kernel(**inputs):
    nc = Prog().build()
    in_maps = make_in_maps(inputs)
    res = run_bass_kernel_spmd(nc, in_maps, core_ids=list(range(NCORES)))
    return np.stack([np.asarray(r["out"], dtype=np.float32) for r in res.results], axis=0)
```

```python
import contextlib
import math
import numpy as np
import concourse.bass as bass
import concourse.mybir as mybir
from concourse.bass_utils import run_bass_kernel_spmd

F32 = mybir.dt.float32
BF16 = mybir.dt.bfloat16
AF = mybir.ActivationFunctionType
ALU = mybir.AluOpType

D = 1024
SEQ = 4096
NCORES = 8
FH = 2816
NM = FH // 128
EPS = 1e-6
NPOOL = 12
T = 256
NS = T // 128
NT = SEQ // T
NCH = SEQ // 128


class Buf:
    __slots__ = ("name", "ws", "rs")

    def __init__(self, name="b"):
        self.name = name
        self.ws = []
        self.rs = []


class Op:
    __slots__ = ("eng", "fn", "deps", "sig", "is_dma", "signaled", "prev")

    def __init__(self, eng, fn, is_dma):
        self.eng = eng
        self.fn = fn
        self.is_dma = is_dma
        self.deps = []
        self.sig = None
        self.signaled = is_dma
        self.prev = None


class Sched:
    ENGS = ("pe", "act", "dve", "pool", "sp")

    def __init__(self, nc):
        self.nc = nc
        self.ops = {e: [] for e in self.ENGS}
        self.last = {}
        self.dmas = []
        self.pending = {}

    def _deps(self, o, reads, writes):
        deps = list(self.pending.pop(o.eng, []))
        for b in reads:
            deps.extend(b.ws)
        for b in writes:
            if o.is_dma and b.ws and not b.rs and all(w.is_dma for w in b.ws):
                continue
            for w in b.ws + b.rs:
                same = (not o.is_dma) and (not w.is_dma) and w.eng == o.eng
                if not same:
                    deps.append(w)
        seen = set()
        for d in deps:
            if id(d) not in seen and d is not o:
                seen.add(id(d))
                d.signaled = True
                o.deps.append(d)
        for b in reads:
            b.rs.append(o)
        for b in writes:
            if o.is_dma and b.ws and not b.rs and all(w.is_dma for w in b.ws):
                b.ws.append(o)
            else:
                b.ws = [o]
                b.rs = []

    def op(self, eng, fn, reads=(), writes=()):
        o = Op(eng, fn, False)
        self._deps(o, reads, writes)
        self.ops[eng].append(o)
        self.last[eng] = o
        return o

    def dma(self, q, out, in_, reads=(), writes=(), slow=False):
        if slow:
            fn = lambda e: e.dma_start(out=out, in_=in_, allow_slow_non_contiguous=True)
        else:
            fn = lambda e: e.dma_start(out=out, in_=in_)
        o = Op(q, fn, True)
        self._deps(o, reads, writes)
        self.ops[q].append(o)
        self.dmas.append(o)
        return o

    def barrier(self):
        deps = [o for o in self.last.values()] + list(self.dmas)
        for d in deps:
            d.signaled = True
        for e in self.ENGS:
            self.pending[e] = list(self.pending.get(e, [])) + [d for d in deps if not (d.eng == e and not d.is_dma)]
        self.dmas = []

    def emit(self):
        nc = self.nc
        sems = {}
        with contextlib.ExitStack() as st:
            for e in ("pe", "act", "dve", "pool"):
                sems[e] = st.enter_context(nc.semaphore("c_" + e))
            pools = {}
            for q in ("sp", "act", "pool"):
                if any(o.is_dma for o in self.ops[q]):
                    pools[q] = [st.enter_context(nc.semaphore("d_%s%d" % (q, i))) for i in range(NPOOL)]
            for e in self.ENGS:
                cnt = 0
                j = 0
                for o in self.ops[e]:
                    if o.is_dma:
                        s = pools[e][j % NPOOL]
                        if j >= NPOOL:
                            o.prev = (s, 16 * (j // NPOOL))
                        o.sig = (s, 16 * (j // NPOOL + 1))
                        j += 1
                    elif o.signaled:
                        cnt += 1
                        o.sig = (sems[e], cnt)
            block = st.enter_context(nc.Block())

            def run(e, eng):
                known = {}
                finals = {}
                for o in self.ops[e]:
                    waits = {}
                    cands = [d.sig for d in o.deps]
                    if o.prev is not None:
                        cands.append(o.prev)
                    for s, v in cands:
                        k = id(s)
                        if k not in waits or waits[k][1] < v:
                            waits[k] = (s, v)
                    for k, (s, v) in waits.items():
                        if known.get(k, 0) < v:
                            eng.wait_ge(s, v)
                            known[k] = v
                    inst = o.fn(eng)
                    if o.is_dma:
                        inst.then_inc(o.sig[0], 16)
                        finals[id(o.sig[0])] = o.sig
                    elif o.signaled:
                        inst.then_inc(o.sig[0], 1)
                for s, v in finals.values():
                    eng.wait_ge(s, v)

            @block.tensor
            def _(eng):
                run("pe", eng)

            @block.scalar
            def _(eng):
                run("act", eng)

            @block.vector
            def _(eng):
                run("dve", eng)

            @block.gpsimd
            def _(eng):
                run("pool", eng)

            @block.sync
            def _(eng):
                run("sp", eng)


class Arena:
    def __init__(self, nc, words, base_name="arena"):
        self.t = nc.alloc_sbuf_tensor(base_name, [128, words], F32)
        self.words = words
        self.off = 0
        self.mark = 0

    def reset(self, to=None):
        self.off = self.mark if to is None else to

    def alloc(self, shape, dt):
        n = int(np.prod(shape))
        w = (n + 1) // 2 if dt == BF16 else n
        w = (w + 7) // 8 * 8
        assert self.off + w <= self.words, "SBUF arena overflow %d + %d > %d" % (self.off, w, self.words)
        ap = self.t[:, self.off:self.off + w]
        self.off += w
        if dt == BF16:
            ap = ap.bitcast(BF16)
        ap = ap[:, 0:n]
        if len(shape) == 2:
            ap = ap.rearrange("p (a b) -> p a b", b=shape[1])
        elif len(shape) == 3:
            ap = ap.rearrange("p (a b c) -> p a b c", b=shape[1], c=shape[2])
        return ap, Buf()


def _gammas():
    gf = [1.0 - 2.0 ** (-5.0 - h) for h in range(4)]
    gb = gf[::-1]
    return gf, gb


def host_tables():
    half = 128
    inv_freq = (1.0 / (np.float32(10000.0) ** np.linspace(0.0, 1.0, half, dtype=np.float32))).astype(np.float32)
    ang = (np.arange(SEQ, dtype=np.float32)[:, None] * inv_freq[None, :]).astype(np.float32)
    cosT = np.ascontiguousarray(np.cos(ang.astype(np.float64)).T).astype(np.float32)
    sinT = np.ascontiguousarray(np.sin(ang.astype(np.float64)).T).astype(np.float32)
    gf, gb = _gammas()
    pos = np.arange(128, dtype=np.float64)
    MT = np.zeros((4, 128, 128), np.float64)
    XF = np.zeros((4, 128, 128), np.float64)
    XB = np.zeros((4, 128, 128), np.float64)
    ZF = np.zeros((128, 4), np.float64)
    ZB = np.zeros((128, 4), np.float64)
    for h in range(4):
        i = pos[None, :]
        j = pos[:, None]
        fwd = np.where(i >= j, gf[h] ** np.maximum(i - j, 0), 0.0)
        bwd = np.where(j > i, gb[h] ** np.maximum(j - i, 0), 0.0)
        MT[h] = (fwd + bwd) / 16.0
        XF[h] = np.broadcast_to(gf[h] ** (pos + 1.0), (128, 128))
        XB[h] = np.broadcast_to(gb[h] ** (128.0 - pos), (128, 128))
        ZF[:, h] = gf[h] ** (127.0 - pos) / 16.0
        ZB[:, h] = gb[h] ** pos / 16.0
    tabs = np.concatenate([MT, XF, XB], axis=0).astype(np.float32)
    tabs = np.ascontiguousarray(tabs.transpose(1, 0, 2))
    zz = np.concatenate([ZF, ZB], axis=1).astype(np.float32)
    return cosT, sinT, tabs, zz


class Prog:
    def __init__(self, dbg=False, phases=None):
        self.dbg = dbg
        self.phases = phases
        nc = self.nc = bass.Bass("TRN2", target_bir_lowering=False)
        self.S = Sched(nc)
        I = lambda name, shape: nc.dram_tensor(name, shape, F32, kind="ExternalInput").ap()
        self.x = I("x", [SEQ, D])
        self.ln_mix = I("ln_mix", [2, D])
        self.ln_ffn = I("ln_ffn", [2, D])
        self.ln_final = I("ln_final", [D])
        self.lru_w_in = I("lru_w_in", [D, 2 * D])
        self.lru_conv_w = I("lru_conv_w", [4, D])
        self.lru_conv_b = I("lru_conv_b", [D])
        self.lru_ga_w = I("lru_gate_a_w", [2, 4, 256, 256])
        self.lru_ga_b = I("lru_gate_a_b", [2, D])
        self.lru_gx_w = I("lru_gate_x_w", [2, 4, 256, 256])
        self.lru_gx_b = I("lru_gate_x_b", [2, D])
        self.lru_lam = I("lru_lambda", [2, D])
        self.lru_w_out = I("lru_w_out", [D, D])
        self.ret_w_in = I("ret_w_in", [D, 6 * D])
        self.ret_w_out = I("ret_w_out", [2 * D, D])
        self.ffn_wg = I("ffn_w_gate", [2, D, FH])
        self.ffn_wu = I("ffn_w_up", [2, D, FH])
        self.ffn_wd = I("ffn_w_down", [2, FH, D])
        self.cosT_d = I("cosT", [128, SEQ])
        self.sinT_d = I("sinT", [128, SEQ])
        self.tabs_d = I("tabs", [128, 12, 128])
        self.zz_d = I("zz", [128, 8])
        kind = "ExternalOutput" if dbg else "Internal"
        self.hsb = nc.dram_tensor("hsb", [D, SEQ], F32, kind=kind).ap()
        self.hm0 = nc.dram_tensor("hm0", [SEQ, D], F32, kind=kind).ap()
        self.h1 = nc.dram_tensor("h1", [SEQ, D], F32, kind=kind).ap()
        self.hm1 = nc.dram_tensor("hm1", [SEQ, D], F32, kind=kind).ap()
        self.sbst = nc.dram_tensor("sbst", [NCH, 128, 8, 512], BF16, kind="Internal").ap()
        self.hnT0d = nc.dram_tensor("hnT0d", [128, 8, SEQ], BF16, kind="Internal").ap()
        self.hnT1d = nc.dram_tensor("hnT1d", [128, 8, SEQ], BF16, kind="Internal").ap()
        self.xbd = nc.dram_tensor("xbd", [D, SEQ], F32, kind="Internal").ap()
        self.kTd = nc.dram_tensor("kTd", [128, 8, SEQ], BF16, kind="Internal").ap()
        self.xbbd = nc.dram_tensor("xbbd", [D, SEQ], BF16, kind="Internal").ap()
        self.vd = nc.dram_tensor("vd", [SEQ, 2 * D], BF16, kind="Internal").ap()
        self.out = nc.dram_tensor("out", [SEQ, D], F32, kind="ExternalOutput").ap()
        self.dbuf = {k: [Buf() for _ in range(NT)] for k in ("hsb", "hm0", "h1", "hm1", "out", "hnT0", "hnT1", "xb", "kT", "v", "xbb")}
        self.dsb = [Buf() for _ in range(NCH)]
        self.ps = [nc.alloc_psum_tensor("ps%d" % i, [128, 512], F32) for i in range(8)]
        self.pb = [Buf() for _ in range(8)]
        self.ar = Arena(nc, 53000)
        self._consts()

    def _consts(self):
        S, ar = self.S, self.ar
        identf, b0 = ar.alloc([128], F32)
        self.ident, self.b_ident = ar.alloc([128], BF16)
        self.mhalf, self.b_mhalf = ar.alloc([8], F32)
        S.op("pool", lambda e: e.memset(identf, 0.0), writes=[b0])
        S.op("pool", lambda e: e.affine_select(out=identf, in_=identf, pattern=[[-1, 128]], compare_op=ALU.not_equal,
                                               fill=1.0, base=0, channel_multiplier=1), reads=[b0], writes=[b0])
        S.op("dve", lambda e: e.tensor_copy(out=self.ident, in_=identf), reads=[b0], writes=[self.b_ident])
        S.op("pool", lambda e: e.memset(self.mhalf, -0.5), writes=[self.b_mhalf])
        ar.mark = ar.off

    def load_gain(self, src_row):
        g, b = self.ar.alloc([D], F32)
        self.S.dma("sp", g, src_row.partition_broadcast(128), writes=[b])
        return g, b

    def rstd_from(self, ssq, b_ssq, n, mult, add):
        S = self.S
        S.op("dve", lambda e: e.tensor_scalar(out=ssq[:, 0:n], in0=ssq[:, 0:n], scalar1=mult, scalar2=add,
                                              op0=ALU.mult, op1=ALU.add), reads=[b_ssq], writes=[b_ssq])
        S.op("pool", lambda e: e.tensor_tensor(out=ssq[:, 0:n], in0=ssq[:, 0:n], in1=self.mhalf[:, 0:n], op=ALU.pow),
             reads=[b_ssq, self.b_mhalf], writes=[b_ssq])

    def norm_T(self, xt, b_xt, g, b_g, hnT, b_hnT, st, banks=(6, 7)):
        self.norm_a(xt, b_xt, g, b_g, st)
        self.norm_b(hnT, b_hnT, st, banks)

    def norm_a(self, xt, b_xt, g, b_g, st):
        S = self.S
        ssq, b_ssq = st["ssq"], st["b_ssq"]
        for s in range(NS):
            S.op("act", lambda e, s=s: e.activation(out=st["junk"], in_=xt[:, s, :], func=AF.Square,
                                                   accum_out=ssq[:, s:s + 1]),
                 reads=[b_xt], writes=[st["b_junk"], b_ssq])
        self.rstd_from(ssq, b_ssq, NS, 1.0 / D, EPS)
        for s in range(NS):
            hn, b_hn = st["hn"][s % 2]
            S.op("dve", lambda e, s=s, hn=hn: e.scalar_tensor_tensor(out=hn, in0=xt[:, s, :], scalar=ssq[:, s:s + 1],
                                                                  in1=g, op0=ALU.mult, op1=ALU.mult),
                 reads=[b_xt, b_ssq, b_g], writes=[b_hn])

    def norm_b(self, hnT, b_hnT, st, banks=(6, 7)):
        S = self.S
        for s in range(NS):
            hn, b_hn = st["hn"][s % 2]
            bank = banks[s % 2]
            psb = self.ps[bank][:].bitcast(BF16)
            for kc in range(8):
                S.op("pe", lambda e, kc=kc, hn=hn, psb=psb: e.transpose(out=psb[:, kc * 128:(kc + 1) * 128],
                                                                        in_=hn[:, kc * 128:(kc + 1) * 128],
                                                                        identity=self.ident),
                     reads=[b_hn, self.b_ident], writes=[self.pb[bank]])
            src = psb.rearrange("p (k t) -> p k t", k=8)
            dst = hnT[:, :, s * 128:(s + 1) * 128]
            if s % 2 == 0:
                S.op("act", lambda e, src=src, dst=dst: e.activation(out=dst, in_=src, func=AF.Copy),
                     reads=[self.pb[bank]], writes=[b_hnT])
            else:
                S.op("dve", lambda e, src=src, dst=dst: e.tensor_copy(out=dst, in_=src),
                     reads=[self.pb[bank]], writes=[b_hnT])

    def norm_state(self):
        ar = self.ar
        st = {}
        st["ssq"], st["b_ssq"] = ar.alloc([8], F32)
        st["junk"], st["b_junk"] = ar.alloc([D], BF16)
        st["hn"] = [ar.alloc([D], BF16) for _ in range(2)]
        return st

    def load_w(self, dst, b_dst, src2d, k_chunks, col0, ncols, piece):
        v = src2d.rearrange("(k p) n -> p k n", p=128)
        for c in range(0, ncols, piece):
            self.S.dma("pool", dst[:, :, c:c + piece], v[:, 0:k_chunks, col0 + c:col0 + c + piece], writes=[b_dst])

    def mm_proj(self, bank, w, b_w, m0, hnT, b_hnT, n=T):
        for kc in range(8):
            self.S.op("pe", lambda e, kc=kc: e.matmul(self.ps[bank][:, 0:n], lhsT=w[:, kc, m0:m0 + 128],
                                                     rhs=hnT[:, kc, 0:n], start=(kc == 0), stop=(kc == 7)),
                      reads=[b_w, b_hnT], writes=[self.pb[bank]])

    def mm_tok(self, bank, act, b_act, s, w, b_w, c0, nk):
        for k in range(nk):
            self.S.op("pe", lambda e, k=k: e.matmul(self.ps[bank][:, 0:512], lhsT=act[:, k, s * 128:(s + 1) * 128],
                                                   rhs=w[:, k, c0:c0 + 512], start=(k == 0), stop=(k == nk - 1)),
                      reads=[b_act, b_w], writes=[self.pb[bank]])

    def phase_ffn(self, layer, src, src_key, dst, dst_key, final):
        S, ar = self.S, self.ar
        S.barrier()
        ar.reset()
        wg, b_wg = ar.alloc([8, FH], BF16)
        wu, b_wu = ar.alloc([8, FH], BF16)
        wd, b_wd = ar.alloc([NM, D], BF16)
        g, b_g = self.load_gain(self.ln_ffn[layer])
        if final:
            gf, b_gf = self.load_gain(self.ln_final)
        b_wgm = [Buf() for _ in range(NM // 2)]
        b_wum = [Buf() for _ in range(NM // 2)]
        vg = self.ffn_wg[layer].rearrange("(k p) n -> p k n", p=128)
        vu = self.ffn_wu[layer].rearrange("(k p) n -> p k n", p=128)
        for mg in range(NM // 2):
            S.dma("pool", wg[:, :, mg * 256:(mg + 1) * 256], vg[:, :, mg * 256:(mg + 1) * 256], writes=[b_wgm[mg]])
            S.dma("pool", wu[:, :, mg * 256:(mg + 1) * 256], vu[:, :, mg * 256:(mg + 1) * 256], writes=[b_wum[mg]])
        vd = self.ffn_wd[layer].rearrange("(k p) n -> p k n", p=128)
        S.dma("pool", wd[:, 0:11, :], vd[:, 0:11, :], writes=[b_wd])
        S.dma("pool", wd[:, 11:22, :], vd[:, 11:22, :], writes=[b_wd])
        st = self.norm_state()
        xts = [ar.alloc([NS, D], F32) for _ in range(2)]
        hnTs = [ar.alloc([8, T], BF16) for _ in range(2)]
        act, b_act = ar.alloc([NM, T], BF16)
        tt = [ar.alloc([T], F32) for _ in range(2)]
        sg = [ar.alloc([T], F32) for _ in range(2)]
        def load_x(j):
            S.dma("sp", xts[j % 2][0], src[j * T:(j + 1) * T, :].rearrange("(s p) d -> p s d", p=128),
                  reads=[self.dbuf[src_key][j]], writes=[xts[j % 2][1]])

        load_x(0)
        self.norm_T(xts[0][0], xts[0][1], g, b_g, hnTs[0][0], hnTs[0][1], st)
        for i in range(NT):
            self.ffn_tile(i, xts, hnTs, st, g, b_g, load_x, locals())

    def ffn_tile(self, i, xts, hnTs, st, g, b_g, load_x, env):
        S = self.S
        wg, wu, wd, b_wgm, b_wum, b_wd = env["wg"], env["wu"], env["wd"], env["b_wgm"], env["b_wum"], env["b_wd"]
        act, b_act, tt, sg, final = env["act"], env["b_act"], env["tt"], env["sg"], env["final"]
        dst, dst_key = env["dst"], env["dst_key"]
        if final:
            gf, b_gf = env["gf"], env["b_gf"]
        if True:
            xt, b_xt = xts[i % 2]
            hnT, b_hnT = hnTs[i % 2]
            if i + 1 < NT:
                load_x(i + 1)
            for m in range(NM):
                if m == 12 and i + 1 < NT:
                    self.norm_a(xts[(i + 1) % 2][0], xts[(i + 1) % 2][1], g, b_g, st)
                bg, bu = ((0, 1), (2, 3), (6, 7))[m % 3]
                self.mm_proj(bg, wg, b_wgm[m // 2], m * 128, hnT, b_hnT)
                self.mm_proj(bu, wu, b_wum[m // 2], m * 128, hnT, b_hnT)
                t_, b_t = tt[m % 2]
                s_, b_s = sg[m % 2]
                pg, pu = self.ps[bg][:, 0:T], self.ps[bu][:, 0:T]
                S.op("act", lambda e, t_=t_, pg=pg: e.activation(out=t_, in_=pg, func=AF.Tanh, scale=0.5),
                     reads=[self.pb[bg]], writes=[b_t])
                S.op("dve", lambda e, t_=t_, s_=s_, pg=pg: e.scalar_tensor_tensor(out=s_, in0=t_, scalar=1.0, in1=pg,
                                                                                 op0=ALU.add, op1=ALU.mult),
                     reads=[b_t, self.pb[bg]], writes=[b_s])
                S.op("dve", lambda e, s_=s_, pu=pu, m=m: e.scalar_tensor_tensor(out=act[:, m, :], in0=s_, scalar=0.5,
                                                                               in1=pu, op0=ALU.mult, op1=ALU.mult),
                     reads=[b_s, self.pb[bu]], writes=[b_act])
            for s in range(NS):
                for n in range(2):
                    bank = 4 + ((s * 2 + n) % 2)
                    self.mm_tok(bank, act, b_act, s, wd, b_wd, n * 512, NM)
                    xs = xt[:, s, n * 512:(n + 1) * 512]
                    S.op("dve", lambda e, xs=xs, bank=bank: e.tensor_tensor(out=xs, in0=xs, in1=self.ps[bank][:, 0:512],
                                                                           op=ALU.add),
                         reads=[self.pb[bank]], writes=[b_xt])
            if final:
                ssq, b_ssq = st["ssq"], st["b_ssq"]
                for s in range(NS):
                    S.op("act", lambda e, s=s, xt=xt: e.activation(out=st["junk"], in_=xt[:, s, :], func=AF.Square,
                                                                  accum_out=ssq[:, 4 + s:5 + s]),
                         reads=[b_xt], writes=[st["b_junk"], b_ssq])
                S.op("dve", lambda e: e.tensor_scalar(out=ssq[:, 4:4 + NS], in0=ssq[:, 4:4 + NS], scalar1=1.0 / D,
                                                      scalar2=EPS, op0=ALU.mult, op1=ALU.add),
                     reads=[b_ssq], writes=[b_ssq])
                S.op("pool", lambda e: e.tensor_tensor(out=ssq[:, 4:4 + NS], in0=ssq[:, 4:4 + NS],
                                                       in1=self.mhalf[:, 0:NS], op=ALU.pow),
                     reads=[b_ssq, self.b_mhalf], writes=[b_ssq])
                for s in range(NS):
                    S.op("dve", lambda e, s=s, xt=xt: e.scalar_tensor_tensor(out=xt[:, s, :], in0=xt[:, s, :],
                                                                     scalar=ssq[:, 4 + s:5 + s], in1=gf,
                                                                     op0=ALU.mult, op1=ALU.mult),
                         reads=[b_ssq, b_gf], writes=[b_xt])
            S.dma("sp", dst[i * T:(i + 1) * T, :].rearrange("(s p) d -> p s d", p=128), xt,
                  reads=[b_xt], writes=[self.dbuf[dst_key][i]])
            if i + 1 < NT:
                self.norm_b(hnTs[(i + 1) % 2][0], hnTs[(i + 1) % 2][1], st)

    def load_chanvec(self, dst, b, src1d):
        self.S.dma("sp", dst, src1d.rearrange("(c p) -> p c", p=128), writes=[b], slow=True)

    def phase_lru(self, bwd):
        S, ar = self.S, self.ar
        S.barrier()
        ar.reset()
        d = 1 if bwd else 0
        ps, pb = self.ps, self.pb
        ga, b_ga = ar.alloc([4, 2, 256], BF16)
        gx, b_gx = ar.alloc([4, 2, 256], BF16)
        S.dma("pool", ga, self.lru_ga_w[d].rearrange("n (k p) j -> p n k j", p=128), writes=[b_ga])
        S.dma("pool", gx, self.lru_gx_w[d].rearrange("n (k p) j -> p n k j", p=128), writes=[b_gx])
        if bwd:
            wx, b_wx = ar.alloc([8, D], BF16)
            self.load_w(wx, b_wx, self.lru_w_in, 8, 0, D, 256)
            g, b_g = self.load_gain(self.ln_mix[0])
            cw = [ar.alloc([8], F32) for _ in range(4)]
            for k in range(4):
                self.load_chanvec(cw[k][0], cw[k][1], self.lru_conv_w[k])
            cb, b_cb = ar.alloc([8], F32)
            self.load_chanvec(cb, b_cb, self.lru_conv_b)
        else:
            wgb, b_wgb = ar.alloc([8, D], BF16)
            self.load_w(wgb, b_wgb, self.lru_w_in, 8, D, D, 256)
            wo, b_wo = ar.alloc([8, D], BF16)
            self.load_w(wo, b_wo, self.lru_w_out, 8, 0, D, 512)
        hba, b_hba = ar.alloc([8], F32)
        hbx, b_hbx = ar.alloc([8], F32)
        lam, b_lam = ar.alloc([8], F32)
        self.load_chanvec(hba, b_hba, self.lru_ga_b[d])
        self.load_chanvec(hbx, b_hbx, self.lru_gx_b[d])
        self.load_chanvec(lam, b_lam, self.lru_lam[d])
        S.op("dve", lambda e: e.tensor_scalar(out=hba, in0=hba, scalar1=0.5, scalar2=None, op0=ALU.mult),
             reads=[b_hba], writes=[b_hba])
        S.op("dve", lambda e: e.tensor_scalar(out=hbx, in0=hbx, scalar1=0.5, scalar2=None, op0=ALU.mult),
             reads=[b_hbx], writes=[b_hbx])
        ee, b_ee = ar.alloc([8], F32)
        pp, b_pp = ar.alloc([8], F32)
        ch, b_ch = ar.alloc([8], F32)
        S.op("act", lambda e: e.activation(out=ee, in_=lam, func=AF.Exp, scale=-1.0), reads=[b_lam], writes=[b_ee])
        S.op("dve", lambda e: e.tensor_scalar(out=pp, in0=ee, scalar1=1.0 / 7, scalar2=-1.0 / 6, op0=ALU.mult, op1=ALU.add),
             reads=[b_ee], writes=[b_pp])
        for cst in (1.0 / 5, -1.0 / 4, 1.0 / 3, -1.0 / 2, 1.0):
            S.op("dve", lambda e: e.tensor_tensor(out=pp, in0=pp, in1=ee, op=ALU.mult), reads=[b_pp, b_ee], writes=[b_pp])
            S.op("dve", lambda e, cst=cst: e.tensor_scalar(out=pp, in0=pp, scalar1=cst, scalar2=None, op0=ALU.add),
                 reads=[b_pp], writes=[b_pp])
        S.op("dve", lambda e: e.scalar_tensor_tensor(out=ch, in0=pp, scalar=-4.0, in1=ee, op0=ALU.mult, op1=ALU.mult),
             reads=[b_pp, b_ee], writes=[b_ch])

        def bufs8():
            return [Buf() for _ in range(8)]

        xbs = [(ar.alloc([8, T], F32)[0], bufs8()) for _ in range(2)]
        xbbs = [(ar.alloc([8, T], BF16)[0], [Buf() for _ in range(4)]) for _ in range(2)]
        Rs = [(ar.alloc([8, T], F32)[0], bufs8()) for _ in range(2)]
        Is = [(ar.alloc([8, T], F32)[0], bufs8()) for _ in range(2)]
        Zs = [(ar.alloc([8, T], F32)[0], bufs8()) for _ in range(2)]
        HS = [ar.alloc([8, T], F32) for _ in range(2)]
        hnTs = [ar.alloc([8, T], BF16) for _ in range(3)]
        if bwd:
            st = self.norm_state()
            xts = [ar.alloc([NS, D], F32) for _ in range(2)]
            pres = [(ar.alloc([8, T + 3], F32)[0], bufs8(), Buf()) for _ in range(3)]
        else:
            ggs = [ar.alloc([8, T], F32) for _ in range(2)]
            gpre, b_gpre = ar.alloc([8, T], F32)
            HBs = [ar.alloc([8, T], F32) for _ in range(2)]
            yT, b_yT = ar.alloc([8, T], BF16)
            xrs = [ar.alloc([NS, D], F32) for _ in range(2)]
        hsb_v = self.hsb.rearrange("(c p) t -> p c t", p=128)
        xbd_v = self.xbd.rearrange("(c p) t -> p c t", p=128)
        xbbd_v = self.xbbd.rearrange("(c p) t -> p c t", p=128)
        order = list(range(NT - 1, -1, -1)) if bwd else list(range(NT))

        def row(tens, i):
            return tens[i * T:(i + 1) * T, :].rearrange("(s p) d -> p s d", p=128)

        def load_x(i):
            xt, b_xt = xts[i % 2]
            S.dma("sp", xt, row(self.x, i), writes=[b_xt])

        def early_b(i):
            xt, b_xt = xts[i % 2]
            hnT, b_hnT = hnTs[i % 3]
            pre, bp, b_halo = pres[i % 3]
            self.norm_T(xt, b_xt, g, b_g, hnT, b_hnT, st)
            S.dma("sp", self.hnT0d[:, :, i * T:(i + 1) * T], hnT, reads=[b_hnT], writes=[self.dbuf["hnT0"][i]])
            for m in range(8):
                bank = m % 2
                self.mm_proj(bank, wx, b_wx, m * 128, hnT, b_hnT)
                if m % 2 == 0:
                    S.op("act", lambda e, m=m, bank=bank: e.activation(out=pre[:, m, 2:T + 2], in_=ps[bank][:, 0:T], func=AF.Copy),
                         reads=[pb[bank]], writes=[bp[m]])
                else:
                    S.op("dve", lambda e, m=m, bank=bank: e.tensor_copy(out=pre[:, m, 2:T + 2], in_=ps[bank][:, 0:T]),
                         reads=[pb[bank]], writes=[bp[m]])

        def conv_start(i):
            pre, bp, b_halo = pres[i % 3]
            if i > 0:
                pl, bpl, _ = pres[(i - 1) % 3]
                S.op("pool", lambda e: e.tensor_copy(out=pre[:, :, 0:2], in_=pl[:, :, T:T + 2]), reads=bpl, writes=[b_halo])
            else:
                S.op("pool", lambda e: e.memset(pre[:, :, 0:2], 0.0), writes=[b_halo])
            if i < NT - 1:
                pr_, bpr, _ = pres[(i + 1) % 3]
                S.op("pool", lambda e: e.tensor_copy(out=pre[:, :, T + 2:T + 3], in_=pr_[:, :, 2:3]), reads=bpr, writes=[b_halo])
            else:
                S.op("pool", lambda e: e.memset(pre[:, :, T + 2:T + 3], 0.0), writes=[b_halo])

        def conv_tap0(i, m):
            pre, bp, b_halo = pres[i % 3]
            xb, bxb = xbs[i % 2]
            S.op("act", lambda e: e.activation(out=xb[:, m, :], in_=pre[:, m, 0:T], func=AF.Identity,
                                               scale=cw[0][0][:, m:m + 1], bias=cb[:, m:m + 1]),
                 reads=[bp[m], b_halo, cw[0][1], b_cb], writes=[bxb[m]])

        def conv_taps(i, m):
            pre, bp, b_halo = pres[i % 3]
            xb, bxb = xbs[i % 2]
            for k in range(1, 4):
                S.op("dve", lambda e, k=k: e.scalar_tensor_tensor(out=xb[:, m, :], in0=pre[:, m, k:k + T],
                                                                 scalar=cw[k][0][:, m:m + 1], in1=xb[:, m, :],
                                                                 op0=ALU.mult, op1=ALU.add),
                     reads=[bp[m], b_halo, cw[k][1], bxb[m]], writes=[bxb[m]])

        def conv_cast(i, n):
            xb, bxb = xbs[i % 2]
            xbb, bxbb = xbbs[i % 2]
            S.op("act", lambda e: e.activation(out=xbb[:, 2 * n:2 * n + 2, :], in_=xb[:, 2 * n:2 * n + 2, :], func=AF.Copy),
                 reads=bxb[2 * n:2 * n + 2], writes=[bxbb[n]])

        def conv_m(i, m):
            conv_tap0(i, m)
            conv_taps(i, m)
            if m % 2 == 1:
                conv_cast(i, m // 2)

        def early_na(i):
            xt, b_xt = xts[i % 2]
            self.norm_a(xt, b_xt, g, b_g, st)

        def early_nb(i):
            hnT, b_hnT = hnTs[i % 3]
            self.norm_b(hnT, b_hnT, st)
            S.dma("sp", self.hnT0d[:, :, i * T:(i + 1) * T], hnT, reads=[b_hnT], writes=[self.dbuf["hnT0"][i]])

        def early_proj(i, m):
            hnT, b_hnT = hnTs[i % 3]
            pre, bp, b_halo = pres[i % 3]
            bank = (0, 1, 4, 5)[m % 4]
            self.mm_proj(bank, wx, b_wx, m * 128, hnT, b_hnT)
            if m % 2 == 0:
                S.op("act", lambda e: e.activation(out=pre[:, m, 2:T + 2], in_=ps[bank][:, 0:T], func=AF.Copy),
                     reads=[pb[bank]], writes=[bp[m]])
            else:
                S.op("dve", lambda e: e.tensor_copy(out=pre[:, m, 2:T + 2], in_=ps[bank][:, 0:T]),
                     reads=[pb[bank]], writes=[bp[m]])

        def conv_end(i):
            xb, bxb = xbs[i % 2]
            xbb, bxbb = xbbs[i % 2]
            S.dma("sp", xbd_v[:, :, i * T:(i + 1) * T], xb, reads=bxb, writes=[self.dbuf["xb"][i]])
            S.dma("sp", xbbd_v[:, :, i * T:(i + 1) * T], xbb, reads=bxbb, writes=[self.dbuf["xbb"][i]])

        def load_f_early(i):
            hnT, b_hnT = hnTs[i % 3]
            xb, bxb = xbs[i % 2]
            xbb, bxbb = xbbs[i % 2]
            S.dma("sp", hnT, self.hnT0d[:, :, i * T:(i + 1) * T], reads=[self.dbuf["hnT0"][i]], writes=[b_hnT])
            S.dma("sp", xbb, xbbd_v[:, :, i * T:(i + 1) * T], reads=[self.dbuf["xbb"][i]], writes=bxbb)
            S.dma("sp", xb, xbd_v[:, :, i * T:(i + 1) * T], reads=[self.dbuf["xb"][i]], writes=bxb)

        def load_f_late(i):
            HB, b_HB = HBs[i % 2]
            xr, b_xr = xrs[i % 2]
            S.dma("sp", HB, hsb_v[:, :, i * T:(i + 1) * T], reads=[self.dbuf["hsb"][i]], writes=[b_HB])
            S.dma("sp", xr, row(self.x, i), writes=[b_xr])

        def gelu_m(i, m):
            hnT, b_hnT = hnTs[i % 3]
            bank = 4 + (m % 2)
            self.mm_proj(bank, wgb, b_wgb, m * 128, hnT, b_hnT)
            S.op("act", lambda e: e.activation(out=gpre[:, m, :], in_=ps[bank][:, 0:T], func=AF.Copy),
                 reads=[pb[bank]], writes=[b_gpre])

        def gelu_a(i):
            gg, b_gg = ggs[i % 2]
            S.op("act", lambda e: e.activation(out=gg, in_=gpre, func=AF.Square), reads=[b_gpre], writes=[b_gg])

        def gelu_b(i):
            gg, b_gg = ggs[i % 2]
            S.op("dve", lambda e: e.tensor_scalar(out=gg, in0=gg, scalar1=0.044715, scalar2=1.0, op0=ALU.mult, op1=ALU.add),
                 reads=[b_gg], writes=[b_gg])
            S.op("dve", lambda e: e.tensor_tensor(out=gg, in0=gg, in1=gpre, op=ALU.mult), reads=[b_gg, b_gpre], writes=[b_gg])
            S.op("act", lambda e: e.activation(out=gg, in_=gg, func=AF.Tanh, scale=0.7978845608028654), reads=[b_gg], writes=[b_gg])

        def gelu_c(i):
            gg, b_gg = ggs[i % 2]
            S.op("dve", lambda e: e.scalar_tensor_tensor(out=gg, in0=gg, scalar=1.0, in1=gpre, op0=ALU.add, op1=ALU.mult),
                 reads=[b_gg, b_gpre], writes=[b_gg])

        def gates(i):
            for m in range(8):
                gates_m(i, m)

        def gates_m(i, m):
            gates_pa(i, m)
            gates_dve(i, m)

        def gates_pa(i, m):
            xbb, bxbb = xbbs[i % 2]
            R, bR = Rs[i % 2]
            Ib, bI = Is[i % 2]
            n, hm = m // 2, m % 2
            br, bi = 2, 3
            for kc in range(2):
                S.op("pe", lambda e, kc=kc: e.matmul(ps[br][:, 0:T], lhsT=ga[:, n, kc, hm * 128:(hm + 1) * 128],
                                                    rhs=xbb[:, 2 * n + kc, :], start=(kc == 0), stop=(kc == 1)),
                     reads=[b_ga, bxbb[n]], writes=[pb[br]])
            for kc in range(2):
                S.op("pe", lambda e, kc=kc: e.matmul(ps[bi][:, 0:T], lhsT=gx[:, n, kc, hm * 128:(hm + 1) * 128],
                                                    rhs=xbb[:, 2 * n + kc, :], start=(kc == 0), stop=(kc == 1)),
                     reads=[b_gx, bxbb[n]], writes=[pb[bi]])
            S.op("act", lambda e: e.activation(out=R[:, m, :], in_=ps[br][:, 0:T], func=AF.Tanh, scale=0.5, bias=hba[:, m:m + 1]),
                 reads=[pb[br], b_hba], writes=[bR[m]])
            S.op("act", lambda e: e.activation(out=Ib[:, m, :], in_=ps[bi][:, 0:T], func=AF.Tanh, scale=0.5, bias=hbx[:, m:m + 1]),
                 reads=[pb[bi], b_hbx], writes=[bI[m]])
            S.op("act", lambda e: e.activation(out=R[:, m, :], in_=R[:, m, :], func=AF.Exp, scale=ch[:, m:m + 1], bias=ch[:, m:m + 1]),
                 reads=[bR[m], b_ch], writes=[bR[m]])

        def gates_dve(i, m):
            xb, bxb = xbs[i % 2]
            R, bR = Rs[i % 2]
            Ib, bI = Is[i % 2]
            Z, bZ = Zs[i % 2]
            S.op("dve", lambda e: e.scalar_tensor_tensor(out=Ib[:, m, :], in0=Ib[:, m, :], scalar=1.0, in1=xb[:, m, :],
                                                         op0=ALU.add, op1=ALU.mult),
                 reads=[bI[m], bxb[m]], writes=[bI[m]])
            S.op("dve", lambda e: e.scalar_tensor_tensor(out=Z[:, m, :], in0=R[:, m, :], scalar=-1.0, in1=R[:, m, :],
                                                         op0=ALU.mult, op1=ALU.mult),
                 reads=[bR[m]], writes=[bZ[m]])

        def scan(i, prev_hs):
            scan_a(i)
            scan_a2(i)
            return scan_b(i, prev_hs)

        def scan_a(i):
            Z, bZ = Zs[i % 2]
            S.op("act", lambda e: e.activation(out=Z, in_=Z, func=AF.Sqrt, scale=1.0, bias=1.0), reads=bZ, writes=bZ)

        def scan_a2(i):
            Ib, bI = Is[i % 2]
            Z, bZ = Zs[i % 2]
            S.op("dve", lambda e: e.scalar_tensor_tensor(out=Z, in0=Z, scalar=0.5, in1=Ib, op0=ALU.mult, op1=ALU.mult),
                 reads=bZ + bI, writes=bZ)

        def scan_b(i, prev_hs):
            for m in range(8):
                scan_m(i, prev_hs, m)
            return HS[i % 2]

        def scan_m(i, prev_hs, m):
            R, bR = Rs[i % 2]
            Z, bZ = Zs[i % 2]
            hs, b_hs = HS[i % 2]
            if True:
                if prev_hs is None:
                    init, rd = 0.0, []
                else:
                    ph, b_ph = prev_hs
                    init = ph[:, m, 0:1] if bwd else ph[:, m, T - 1:T]
                    rd = [b_ph]
                if bwd:
                    o_, a_, z_ = hs[:, m, ::-1], R[:, m, ::-1], Z[:, m, ::-1]
                else:
                    o_, a_, z_ = hs[:, m, :], R[:, m, :], Z[:, m, :]
                S.op("dve", lambda e, o_=o_, a_=a_, z_=z_, init=init: e.tensor_tensor_scan(out=o_, data0=a_, data1=z_,
                                                                                          initial=init, op0=ALU.mult,
                                                                                          op1=ALU.add),
                     reads=[bR[m], bZ[m]] + rd, writes=[b_hs])
            return (hs, b_hs)

        def finish_f(i, hsb_):
            hs, b_hs = hsb_
            HB, b_HB = HBs[i % 2]
            xr, b_xr = xrs[i % 2]
            gg, b_gg = ggs[i % 2]
            S.op("dve", lambda e: e.tensor_tensor(out=HB, in0=HB, in1=hs, op=ALU.add), reads=[b_HB, b_hs], writes=[b_HB])
            S.op("dve", lambda e: e.scalar_tensor_tensor(out=yT, in0=HB, scalar=0.5, in1=gg, op0=ALU.mult, op1=ALU.mult),
                 reads=[b_HB, b_gg], writes=[b_yT])
            for s in range(NS):
                for n in range(2):
                    bank = 4 + s * 2 + n
                    self.mm_tok(bank, yT, b_yT, s, wo, b_wo, n * 512, 8)

        def finish_b(i):
            xr, b_xr = xrs[i % 2]
            for s in range(NS):
                for n in range(2):
                    bank = 4 + s * 2 + n
                    xs = xr[:, s, n * 512:(n + 1) * 512]
                    S.op("dve", lambda e, xs=xs, bank=bank: e.tensor_tensor(out=xs, in0=xs, in1=ps[bank][:, 0:512], op=ALU.add),
                         reads=[pb[bank]], writes=[b_xr])
            S.dma("sp", row(self.hm0, i), xr, reads=[b_xr], writes=[self.dbuf["hm0"][i]])

        n_ = len(order)
        prev_hs = None
        if bwd:
            def E(k):
                early_b(order[k])
                if k + 2 < n_:
                    load_x(order[k + 2])

            def SC(k):
                j = order[k]
                hsb_ = scan(j, prev_hs_box[0])
                prev_hs_box[0] = hsb_
                S.dma("sp", hsb_v[:, :, j * T:(j + 1) * T], hsb_[0], reads=[hsb_[1]], writes=[self.dbuf["hsb"][j]])

            prev_hs_box = [None]
            load_x(order[0])
            load_x(order[1])
            E(0)
            E(1)
            E(2)
            conv_start(order[0])
            for m in range(8):
                conv_m(order[0], m)
            conv_end(order[0])
            for t in range(n_):
                gi = order[t]
                cj = order[t + 1] if t + 1 < n_ else None
                sj = order[t - 1] if t >= 1 else None
                ej = order[t + 3] if t + 3 < n_ else None
                if cj is not None:
                    conv_start(cj)
                if ej is not None:
                    early_na(ej)
                if sj is not None:
                    scan_a(sj)
                for k in range(10):
                    if cj is not None and k < 8:
                        conv_tap0(cj, k)
                    if cj is not None and k >= 3 and k % 2 == 1:
                        conv_cast(cj, (k - 3) // 2)
                    if k < 8:
                        gates_pa(gi, k)
                    if ej is not None:
                        if k == 1:
                            early_nb(ej)
                        elif k >= 2:
                            early_proj(ej, k - 2)
                    if cj is not None and 1 <= k <= 8:
                        conv_taps(cj, k - 1)
                    if 1 <= k <= 8:
                        gates_dve(gi, k - 1)
                    if sj is not None:
                        if k == 0:
                            scan_a2(sj)
                        elif k <= 8:
                            scan_m(sj, prev_hs_box[0], k - 1)
                if cj is not None:
                    conv_end(cj)
                if sj is not None:
                    hsb_ = HS[sj % 2]
                    prev_hs_box[0] = hsb_
                    S.dma("sp", hsb_v[:, :, sj * T:(sj + 1) * T], hsb_[0], reads=[hsb_[1]], writes=[self.dbuf["hsb"][sj]])
                if t + 5 < n_:
                    load_x(order[t + 5])
            SC(n_ - 1)
        else:
            load_f_early(0)
            load_f_late(0)
            load_f_early(1)
            for idx in range(n_):
                i = order[idx]
                j = order[idx - 1] if idx >= 1 else None
                if j is not None:
                    scan_a(j)
                for k in range(10):
                    if k < 8:
                        gelu_m(i, k)
                        gates_pa(i, k)
                    if 1 <= k <= 8:
                        gates_dve(i, k - 1)
                    if j is not None:
                        if k == 0:
                            scan_a2(j)
                        elif k <= 8:
                            scan_m(j, prev_hs, k - 1)
                if j is not None:
                    prev_hs = HS[j % 2]
                    finish_f(j, prev_hs)
                gelu_a(i)
                gelu_b(i)
                gelu_c(i)
                if j is not None:
                    finish_b(j)
                if i + 1 < NT:
                    load_f_late(i + 1)
                if i + 2 < NT:
                    load_f_early(i + 2)
            j = order[-1]
            prev_hs = scan(j, prev_hs)
            finish_f(j, prev_hs)
            finish_b(j)

    def rotary(self, pre, b_pre, outT, b_out, c0, t0, cosT, sinT, b_cs, tmp, nh=1):
        S = self.S
        (ta, b_ta), (tb, b_tb), (tc, b_tc), (td, b_td) = tmp
        x1, x2 = pre[:, c0:c0 + 2 * nh:2, :], pre[:, c0 + 1:c0 + 2 * nh:2, :]
        o1, o2 = outT[:, c0:c0 + 2 * nh:2, :], outT[:, c0 + 1:c0 + 2 * nh:2, :]
        co = cosT[:, t0:t0 + T].unsqueeze(1).broadcast_to([128, nh, T])
        si = sinT[:, t0:t0 + T].unsqueeze(1).broadcast_to([128, nh, T])
        S.op("dve", lambda e: e.tensor_tensor(out=ta, in0=x1, in1=co, op=ALU.mult), reads=[b_pre, b_cs], writes=[b_ta])
        S.op("dve", lambda e: e.tensor_tensor(out=tb, in0=x2, in1=si, op=ALU.mult), reads=[b_pre, b_cs], writes=[b_tb])
        S.op("dve", lambda e: e.tensor_tensor(out=tc, in0=x2, in1=co, op=ALU.mult), reads=[b_pre, b_cs], writes=[b_tc])
        S.op("dve", lambda e: e.tensor_tensor(out=td, in0=x1, in1=si, op=ALU.mult), reads=[b_pre, b_cs], writes=[b_td])
        S.op("dve", lambda e: e.tensor_tensor(out=o1, in0=ta, in1=tb, op=ALU.subtract), reads=[b_ta, b_tb], writes=[b_out])
        S.op("dve", lambda e: e.tensor_tensor(out=o2, in0=tc, in1=td, op=ALU.add), reads=[b_tc, b_td], writes=[b_out])

    def load_tables(self):
        ar, S = self.ar, self.S
        cosT, b_cs = ar.alloc([SEQ], F32)
        sinT, _ = ar.alloc([SEQ], F32)
        S.dma("sp", cosT, self.cosT_d, writes=[b_cs])
        S.dma("sp", sinT, self.sinT_d, writes=[b_cs])
        zz, b_zz = ar.alloc([8], F32)
        S.dma("sp", zz, self.zz_d, writes=[b_zz])
        return cosT, sinT, b_cs, zz, b_zz

    def phase_ret_bwd(self):
        S, ar = self.S, self.ar
        S.barrier()
        ar.reset()
        gf, gb = _gammas()
        ps, pb = self.ps, self.pb
        wk, b_wk = ar.alloc([8, D], BF16)
        self.load_w(wk, b_wk, self.ret_w_in, 8, D, D, 256)
        wv, b_wv = ar.alloc([8, 2 * D], BF16)
        self.load_w(wv, b_wv, self.ret_w_in, 8, 2 * D, 2 * D, 512)
        g, b_g = self.load_gain(self.ln_mix[1])
        zz, b_zz = ar.alloc([8], F32)
        S.dma("sp", zz, self.zz_d, writes=[b_zz])
        csts = [(ar.alloc([T], F32)[0], ar.alloc([T], F32)[0], Buf()) for _ in range(3)]
        st = self.norm_state()
        xts = [ar.alloc([NS, D], F32) for _ in range(3)]
        hnTs = [ar.alloc([8, T], BF16) for _ in range(2)]
        kpres = [ar.alloc([8, T], F32) for _ in range(2)]
        kTs = [ar.alloc([8, T], BF16) for _ in range(2)]
        vs = [ar.alloc([NS, 2 * D], BF16) for _ in range(2)]
        Sbs = [ar.alloc([8, 512], F32) for _ in range(2)]
        cc_box = [0]
        SbB = [ar.alloc([8, 512], BF16) for _ in range(2)]
        kbs = [ar.alloc([D], BF16) for _ in range(2)]
        tmp = [ar.alloc([4, T], F32) for _ in range(4)]
        S.op("pool", lambda e: e.memset(Sbs[0][0], 0.0), writes=[Sbs[0][1]])

        def row(tens, i):
            return tens[i * T:(i + 1) * T, :].rearrange("(s p) d -> p s d", p=128)

        def load_x(i):
            S.dma("sp", xts[i % 3][0], row(self.h1, i), reads=[self.dbuf["h1"][i]], writes=[xts[i % 3][1]])
            co, si, b_c = csts[i % 3]
            S.dma("sp", co, self.cosT_d[:, i * T:(i + 1) * T], writes=[b_c])
            S.dma("sp", si, self.sinT_d[:, i * T:(i + 1) * T], writes=[b_c])

        def na(i):
            self.norm_a(xts[i % 3][0], xts[i % 3][1], g, b_g, st)

        def nb(i):
            hnT, b_hnT = hnTs[i % 2]
            self.norm_b(hnT, b_hnT, st)
            S.dma("sp", self.hnT1d[:, :, i * T:(i + 1) * T], hnT, reads=[b_hnT], writes=[self.dbuf["hnT1"][i]])

        def PK(i):
            hnT, b_hnT = hnTs[i % 2]
            kpre, b_kpre = kpres[i % 2]
            kT, b_kT = kTs[i % 2]
            for m in range(8):
                bank = m % 2
                self.mm_proj(bank, wk, b_wk, m * 128, hnT, b_hnT)
                S.op("act", lambda e, m=m, bank=bank: e.activation(out=kpre[:, m, :], in_=ps[bank][:, 0:T], func=AF.Copy),
                     reads=[pb[bank]], writes=[b_kpre])
            co, si, b_c = csts[i % 3]
            self.rotary(kpre, b_kpre, kT, b_kT, 0, 0, co, si, b_c, tmp, nh=4)
            S.dma("sp", self.kTd[:, :, i * T:(i + 1) * T], kT, reads=[b_kT], writes=[self.dbuf["kT"][i]])

        def PV(i, s):
            hnT, b_hnT = hnTs[i % 2]
            v, b_v = vs[i % 2]
            for n in range(4):
                bank = n % 2
                self.mm_tok(bank, hnT, b_hnT, s, wv, b_wv, n * 512, 8)
                S.op("act", lambda e, n=n, bank=bank: e.activation(out=v[:, s, n * 512:(n + 1) * 512], in_=ps[bank][:, 0:512], func=AF.Copy),
                     reads=[pb[bank]], writes=[b_v])

        def store_v(i):
            v, b_v = vs[i % 2]
            S.dma("sp", self.vd[i * T:(i + 1) * T, :].rearrange("(s p) e -> p s e", p=128), v, reads=[b_v], writes=[self.dbuf["v"][i]])

        def CH(i, s):
            c = i * NS + s
            kT, b_kT = kTs[i % 2]
            v, b_v = vs[i % 2]
            sbb, b_sbb = SbB[c % 2]
            kb, b_kb = kbs[c % 2]
            Sb, b_Sb = Sbs[cc_box[0] % 2]
            Sn, b_Sn = Sbs[(cc_box[0] + 1) % 2]
            cc_box[0] += 1
            S.op("act", lambda e: e.activation(out=sbb, in_=Sb, func=AF.Copy), reads=[b_Sb], writes=[b_sbb])
            S.dma("sp", self.sbst[c], sbb, reads=[b_sbb], writes=[self.dsb[c]])
            bank = 6 + (c % 2)
            psb = ps[bank][:].bitcast(BF16)
            for m in range(8):
                S.op("pe", lambda e, m=m: e.transpose(out=psb[:, m * 128:(m + 1) * 128], in_=kT[:, m, s * 128:(s + 1) * 128],
                                                      identity=self.ident), reads=[b_kT, self.b_ident], writes=[pb[bank]])
            for h in range(4):
                S.op("dve", lambda e, h=h: e.tensor_scalar(out=kb[:, h * 256:(h + 1) * 256], in0=psb[:, h * 256:(h + 1) * 256],
                                                          scalar1=zz[:, 4 + h:5 + h], scalar2=None, op0=ALU.mult),
                     reads=[pb[bank], b_zz], writes=[b_kb])
            for h in range(4):
                g128 = float(gb[h] ** 128)
                for dc in range(2):
                    bk = 2 + ((h * 2 + dc) % 4)
                    S.op("pe", lambda e, h=h, dc=dc, bk=bk: e.matmul(ps[bk][:, 0:512], lhsT=kb[:, h * 256 + dc * 128:h * 256 + (dc + 1) * 128],
                                                                    rhs=v[:, s, h * 512:(h + 1) * 512], start=True, stop=True),
                         reads=[b_kb, b_v], writes=[pb[bk]])
                    S.op("dve", lambda e, h=h, dc=dc, bk=bk, g128=g128: e.scalar_tensor_tensor(
                        out=Sn[:, h * 2 + dc, :], in0=Sb[:, h * 2 + dc, :], scalar=g128, in1=ps[bk][:, 0:512],
                        op0=ALU.mult, op1=ALU.add), reads=[pb[bk], b_Sb], writes=[b_Sn])

        assert NS == 2
        order = list(range(NT - 1, -1, -1))
        n_ = len(order)
        load_x(order[0])
        load_x(order[1])
        load_x(order[2])
        na(order[0])
        nb(order[0])
        PK(order[0])
        PV(order[0], 1)
        PV(order[0], 0)
        store_v(order[0])
        na(order[1])
        for idx in range(n_):
            i = order[idx]
            nxt = order[idx + 1] if idx + 1 < n_ else None
            nxt2 = order[idx + 2] if idx + 2 < n_ else None
            CH(i, 1)
            if nxt is not None:
                nb(nxt)
                PK(nxt)
            if nxt2 is not None:
                na(nxt2)
            if idx + 3 < n_:
                load_x(order[idx + 3])
            if nxt is not None:
                PV(nxt, 1)
                PV(nxt, 0)
                store_v(nxt)
            CH(i, 0)

    def phase_ret_fwd_all(self):
        S, ar = self.S, self.ar
        S.barrier()
        ar.reset()
        gf, gb = _gammas()
        ps, pb = self.ps, self.pb
        cosT, sinT, b_cs, zz, b_zz = self.load_tables()
        W = []
        for par in range(2):
            wq, b_wq = ar.alloc([8, 256], BF16)
            wgt, b_wgt = ar.alloc([8, 512], BF16)
            wo, b_wo = ar.alloc([4, D], BF16)
            MT, b_tabs = ar.alloc([128], F32)
            XF2, _ = ar.alloc([T], F32)
            XB2, _ = ar.alloc([T], F32)
            W.append((wq, b_wq, wgt, b_wgt, wo, b_wo, MT, b_tabs, XF2, XB2))

        def load_head(h):
            wq, b_wq, wgt, b_wgt, wo, b_wo, MT, b_tabs, XF2, XB2 = W[h % 2]
            self.load_w(wq, b_wq, self.ret_w_in, 8, h * 256, 256, 256)
            self.load_w(wgt, b_wgt, self.ret_w_in, 8, 4 * D + h * 512, 512, 512)
            S.dma("pool", wo, self.ret_w_out[h * 512:(h + 1) * 512, :].rearrange("(k p) n -> p k n", p=128), writes=[b_wo])
            S.dma("sp", MT, self.tabs_d[:, h, :], writes=[b_tabs])
            for s in range(NS):
                S.dma("sp", XF2[:, s * 128:(s + 1) * 128], self.tabs_d[:, 4 + h, :], writes=[b_tabs])
                S.dma("sp", XB2[:, s * 128:(s + 1) * 128], self.tabs_d[:, 8 + h, :], writes=[b_tabs])

        hacc_s = [ar.alloc([NS, D], F32) for _ in range(3)]
        hnTs = [ar.alloc([8, T], BF16) for _ in range(3)]
        qps = [ar.alloc([2, T], F32) for _ in range(2)]
        qTs = [ar.alloc([2, T], BF16) for _ in range(2)]
        kTs = [ar.alloc([2, T], BF16) for _ in range(2)]
        vs = [ar.alloc([NS, 512], BF16) for _ in range(2)]
        sgs = [ar.alloc([NS, 512], F32) for _ in range(2)]
        SMs = [ar.alloc([NS, 128], BF16) for _ in range(2)]
        kfs = [ar.alloc([NS, 256], BF16) for _ in range(2)]
        qfs = [ar.alloc([2, T], BF16) for _ in range(2)]
        qbs = [ar.alloc([2, T], BF16) for _ in range(2)]
        tgs = [ar.alloc([512], F32) for _ in range(2)]
        Sf, b_Sf = ar.alloc([2, 512], F32)
        SfB = [ar.alloc([2, 512], BF16) for _ in range(2)]
        SbL = [ar.alloc([2, 512], BF16) for _ in range(4)]
        zs = [ar.alloc([512], BF16) for _ in range(2)]
        zTs = [ar.alloc([4, 128], BF16) for _ in range(2)]
        rss = [ar.alloc([8], F32) for _ in range(2)]
        junk2, b_junk2 = ar.alloc([512], BF16)
        tmp = [ar.alloc([1, T], F32) for _ in range(4)]
        SfZ, b_SfZ = ar.alloc([2, 512], BF16)
        S.op("pool", lambda e: e.memset(SfZ, 0.0), writes=[b_SfZ])
        NG = 4 * NT

        def row(tens, i):
            return tens[i * T:(i + 1) * T, :].rearrange("(s p) d -> p s d", p=128)

        def L(gt):
            h, i = divmod(gt, NT)
            hnT, b_hnT = hnTs[gt % 3]
            kT, b_kT = kTs[gt % 2]
            v, b_v = vs[gt % 2]
            ha, b_ha = hacc_s[gt % 3]
            S.dma("sp", hnT, self.hnT1d[:, :, i * T:(i + 1) * T], reads=[self.dbuf["hnT1"][i]], writes=[b_hnT])
            S.dma("sp", kT, self.kTd[:, 2 * h:2 * h + 2, i * T:(i + 1) * T], reads=[self.dbuf["kT"][i]], writes=[b_kT])
            S.dma("sp", v, self.vd[i * T:(i + 1) * T, h * 512:(h + 1) * 512].rearrange("(s p) e -> p s e", p=128),
                  reads=[self.dbuf["v"][i]], writes=[b_v])
            if h > 0:
                S.dma("sp", ha, row(self.hm1, i), reads=[self.dbuf["hm1"][i]], writes=[b_ha])
            else:
                S.dma("sp", ha, row(self.h1, i), reads=[self.dbuf["h1"][i]], writes=[b_ha])

        def load_sb(gc):
            h, c = divmod(gc, NCH)
            sbl, b_sbl = SbL[gc % 4]
            S.dma("sp", sbl, self.sbst[c][:, 2 * h:2 * h + 2, :], reads=[self.dsb[c]], writes=[b_sbl])

        def P1(gt):
            h, i = divmod(gt, NT)
            wq, b_wq, wgt, b_wgt, wo, b_wo, MT, b_tabs, XF2, XB2 = W[h % 2]
            hnT, b_hnT = hnTs[gt % 3]
            qp, b_qp = qps[gt % 2]
            qT, b_qT = qTs[gt % 2]
            qf, b_qf = qfs[gt % 2]
            qb, b_qb = qbs[gt % 2]
            for dc in range(2):
                for kc in range(8):
                    S.op("pe", lambda e, kc=kc, dc=dc: e.matmul(ps[dc][:, 0:T], lhsT=wq[:, kc, dc * 128:(dc + 1) * 128],
                                                               rhs=hnT[:, kc, :], start=(kc == 0), stop=(kc == 7)),
                         reads=[b_wq, b_hnT], writes=[pb[dc]])
                S.op("act", lambda e, dc=dc: e.activation(out=qp[:, dc, :], in_=ps[dc][:, 0:T], func=AF.Copy),
                     reads=[pb[dc]], writes=[b_qp])
            self.rotary(qp, b_qp, qT, b_qT, 0, i * T, cosT, sinT, b_cs, tmp)
            for dc in range(2):
                S.op("dve", lambda e, dc=dc: e.tensor_tensor(out=qf[:, dc, :], in0=qT[:, dc, :], in1=XF2, op=ALU.mult),
                     reads=[b_qT, b_tabs], writes=[b_qf])
                S.op("dve", lambda e, dc=dc: e.tensor_tensor(out=qb[:, dc, :], in0=qT[:, dc, :], in1=XB2, op=ALU.mult),
                     reads=[b_qT, b_tabs], writes=[b_qb])

        def P2(gt, s):
            h, i = divmod(gt, NT)
            wq, b_wq, wgt, b_wgt, wo, b_wo, MT, b_tabs, XF2, XB2 = W[h % 2]
            hnT, b_hnT = hnTs[gt % 3]
            sgate, b_sgate = sgs[gt % 2]
            tg, b_tg = tgs[s % 2]
            for k in range(8):
                S.op("pe", lambda e, k=k: e.matmul(ps[0][:, 0:512], lhsT=hnT[:, k, s * 128:(s + 1) * 128], rhs=wgt[:, k, :],
                                                  start=(k == 0), stop=(k == 7)), reads=[b_hnT, b_wgt], writes=[pb[0]])
            S.op("act", lambda e: e.activation(out=tg, in_=ps[0][:, 0:512], func=AF.Tanh, scale=0.5), reads=[pb[0]], writes=[b_tg])
            S.op("dve", lambda e: e.scalar_tensor_tensor(out=sgate[:, s, :], in0=tg, scalar=1.0, in1=ps[0][:, 0:512],
                                                         op0=ALU.add, op1=ALU.mult),
                 reads=[b_tg, pb[0]], writes=[b_sgate])

        def P3(gt):
            h, i = divmod(gt, NT)
            wq, b_wq, wgt, b_wgt, wo, b_wo, MT, b_tabs, XF2, XB2 = W[h % 2]
            qT, b_qT = qTs[gt % 2]
            kT, b_kT = kTs[gt % 2]
            SM, b_SM = SMs[gt % 2]
            kf, b_kf = kfs[gt % 2]
            psb6 = ps[6][:].bitcast(BF16)
            for s in range(NS):
                cs = slice(s * 128, (s + 1) * 128)
                for dc in range(2):
                    S.op("pe", lambda e, dc=dc, cs=cs, s=s: e.transpose(out=psb6[:, s * 256 + dc * 128:s * 256 + (dc + 1) * 128],
                                                                       in_=kT[:, dc, cs], identity=self.ident),
                         reads=[b_kT, self.b_ident], writes=[pb[6]])
            for s in range(NS):
                cs = slice(s * 128, (s + 1) * 128)
                for dc in range(2):
                    S.op("pe", lambda e, dc=dc, cs=cs, s=s: e.matmul(ps[2][:, s * 128:(s + 1) * 128], lhsT=kT[:, dc, cs], rhs=qT[:, dc, cs],
                                                                    start=(dc == 0), stop=(dc == 1)), reads=[b_kT, b_qT], writes=[pb[2]])
            S.op("dve", lambda e: e.tensor_scalar(out=kf, in0=psb6[:, 0:NS * 256].rearrange("p (s d) -> p s d", s=NS), scalar1=zz[:, h:h + 1],
                                                  scalar2=None, op0=ALU.mult), reads=[pb[6], b_zz], writes=[b_kf])
            for s in range(NS):
                S.op("dve", lambda e, s=s: e.tensor_tensor(out=SM[:, s, :], in0=ps[2][:, s * 128:(s + 1) * 128], in1=MT, op=ALU.mult),
                     reads=[pb[2], b_tabs], writes=[b_SM])

        def C(gt, s):
            h, i = divmod(gt, NT)
            g128 = float(gf[h] ** 128)
            c = i * NS + s
            gc = h * NCH + c
            cp = gc % 2
            cs = slice(s * 128, (s + 1) * 128)
            v, b_v = vs[gt % 2]
            sgate, b_sgate = sgs[gt % 2]
            SM, b_SM = SMs[gt % 2]
            kf, b_kf = kfs[gt % 2]
            qf, b_qf = qfs[gt % 2]
            qb, b_qb = qbs[gt % 2]
            sfb, b_sfb = (SfZ, b_SfZ) if c == 0 else SfB[cp]
            sfb_n, b_sfb_n = SfB[1 - cp]
            sbl, b_sbl = SbL[gc % 4]
            z, b_z = zs[cp]
            rs, b_rs = rss[cp]
            yb = 3 + cp
            for dc in range(2):
                bank = 5 if dc == 0 else 1
                S.op("pe", lambda e, dc=dc, bank=bank: e.matmul(ps[bank][:, 0:512], lhsT=kf[:, s, dc * 128:(dc + 1) * 128],
                                                               rhs=v[:, s, :], start=True, stop=True),
                     reads=[b_kf, b_v], writes=[pb[bank]])
                if c == 0:
                    S.op("dve", lambda e, dc=dc, bank=bank: e.tensor_copy(out=Sf[:, dc, :], in_=ps[bank][:, 0:512]),
                         reads=[pb[bank]], writes=[b_Sf])
                else:
                    S.op("dve", lambda e, dc=dc, bank=bank: e.scalar_tensor_tensor(out=Sf[:, dc, :], in0=Sf[:, dc, :], scalar=g128,
                                                                                  in1=ps[bank][:, 0:512], op0=ALU.mult, op1=ALU.add),
                         reads=[pb[bank]], writes=[b_Sf])
            S.op("pe", lambda e: e.matmul(ps[yb][:, 0:512], lhsT=SM[:, s, :], rhs=v[:, s, :], start=True, stop=False),
                 reads=[b_SM, b_v], writes=[pb[yb]])
            for dc in range(2):
                S.op("pe", lambda e, dc=dc: e.matmul(ps[yb][:, 0:512], lhsT=qf[:, dc, cs], rhs=sfb[:, dc, :], start=False, stop=False),
                     reads=[b_qf, b_sfb], writes=[pb[yb]])
            for dc in range(2):
                S.op("pe", lambda e, dc=dc: e.matmul(ps[yb][:, 0:512], lhsT=qb[:, dc, cs], rhs=sbl[:, dc, :], start=False, stop=(dc == 1)),
                     reads=[b_qb, b_sbl], writes=[pb[yb]])
            S.op("act", lambda e: e.activation(out=sfb_n, in_=Sf, func=AF.Copy), reads=[b_Sf], writes=[b_sfb_n])
            S.op("dve", lambda e: e.tensor_tensor(out=z, in0=ps[yb][:, 0:512], in1=sgate[:, s, :], op=ALU.mult),
                 reads=[pb[yb], b_sgate], writes=[b_z])
            S.op("act", lambda e: e.activation(out=junk2, in_=ps[yb][:, 0:512], func=AF.Square, accum_out=rs[:, 0:1]),
                 reads=[pb[yb], b_z], writes=[b_junk2, b_rs])
            self.rstd_from(rs, b_rs, 1, 4.0 / 512, 4.0 * EPS)
            if gc + 2 < 4 * NCH:
                load_sb(gc + 2)

        def S2a(gt, s):
            h, i = divmod(gt, NT)
            gc = h * NCH + i * NS + s
            cp = gc % 2
            z, b_z = zs[cp]
            zT, b_zT = zTs[cp]
            psb7 = ps[7][:].bitcast(BF16)
            for ec in range(4):
                S.op("pe", lambda e, ec=ec: e.transpose(out=psb7[:, ec * 128:(ec + 1) * 128], in_=z[:, ec * 128:(ec + 1) * 128],
                                                        identity=self.ident), reads=[b_z, self.b_ident], writes=[pb[7]])
            S.op("act", lambda e: e.activation(out=zT, in_=psb7[:, 0:512].rearrange("p (k t) -> p k t", k=4), func=AF.Copy),
                 reads=[pb[7]], writes=[b_zT])

        def S2b(gt, s):
            h, i = divmod(gt, NT)
            wq, b_wq, wgt, b_wgt, wo, b_wo, MT, b_tabs, XF2, XB2 = W[h % 2]
            gc = h * NCH + i * NS + s
            cp = gc % 2
            zT, b_zT = zTs[cp]
            rs, b_rs = rss[cp]
            hacc, b_hacc = hacc_s[gt % 3]
            for n in range(2):
                bank = 7 if n == 0 else 3 + cp
                for ec in range(4):
                    S.op("pe", lambda e, ec=ec, n=n, bank=bank: e.matmul(ps[bank][:, 0:512], lhsT=zT[:, ec, :],
                                                                         rhs=wo[:, ec, n * 512:(n + 1) * 512],
                                                                         start=(ec == 0), stop=(ec == 3)),
                         reads=[b_zT, b_wo], writes=[pb[bank]])
                hs_ = hacc[:, s, n * 512:(n + 1) * 512]
                S.op("dve", lambda e, hs_=hs_, bank=bank: e.scalar_tensor_tensor(out=hs_, in0=ps[bank][:, 0:512], scalar=rs[:, 0:1],
                                                                                in1=hs_, op0=ALU.mult, op1=ALU.add),
                     reads=[pb[bank], b_rs], writes=[b_hacc])
            if s == NS - 1:
                S.dma("sp", row(self.hm1, i), hacc, reads=[b_hacc], writes=[self.dbuf["hm1"][i]])

        assert NS == 2
        load_head(0)
        load_head(1)
        L(0)
        L(1)
        load_sb(0)
        load_sb(1)
        P1(0)
        P2(0, 0)
        P2(0, 1)
        for gt in range(NG):
            h, i = divmod(gt, NT)
            nxt = gt + 1 < NG
            P3(gt)
            if gt > 0:
                S2b(gt - 1, 1)
            if i == 0 and 1 <= h and h + 1 < 4:
                load_head(h + 1)
            C(gt, 0)
            if nxt:
                P1(gt + 1)
            C(gt, 1)
            S2a(gt, 0)
            if nxt:
                P2(gt + 1, 0)
            S2b(gt, 0)
            if nxt:
                P2(gt + 1, 1)
            S2a(gt, 1)
            if gt + 2 < NG:
                L(gt + 2)
        S2b(NG - 1, 1)

    def build(self):
        ph = self.phases
        on = lambda p: ph is None or p in ph
        if on("A1"):
            self.phase_lru(True)
        if on("A2"):
            self.phase_lru(False)
        if on("F0"):
            self.phase_ffn(0, self.hm0, "hm0", self.h1, "h1", False)
        if on("R1"):
            self.phase_ret_bwd()
        if on("R20") or on("R23"):
            self.phase_ret_fwd_all()
        if on("F1"):
            self.phase_ffn(1, self.hm1, "hm1", self.out, "out", True)
        self.S.emit()
        return self.nc


_CONST = None


def make_in_maps(inputs, n=NCORES):
    global _CONST
    if _CONST is None:
        _CONST = host_tables()
    cosT, sinT, tabs, zz = _CONST
    f = lambda a: np.ascontiguousarray(np.asarray(a, dtype=np.float32))
    shared = {
        "ln_mix": f(inputs["ln_mix"]), "ln_ffn": f(inputs["ln_ffn"]), "ln_final": f(inputs["ln_final"]),
        "lru_w_in": f(inputs["lru_w_in"][0]), "lru_conv_w": f(inputs["lru_conv_w"][0]),
        "lru_conv_b": f(inputs["lru_conv_b"][0]), "lru_gate_a_w": f(inputs["lru_gate_a_w"][0]),
        "lru_gate_a_b": f(inputs["lru_gate_a_b"][0]), "lru_gate_x_w": f(inputs["lru_gate_x_w"][0]),
        "lru_gate_x_b": f(inputs["lru_gate_x_b"][0]), "lru_lambda": f(inputs["lru_lambda"][0]),
        "lru_w_out": f(inputs["lru_w_out"][0]), "ret_w_in": f(inputs["ret_w_in"][0]),
        "ret_w_out": f(inputs["ret_w_out"][0]), "ffn_w_gate": f(inputs["ffn_w_gate"]),
        "ffn_w_up": f(inputs["ffn_w_up"]), "ffn_w_down": f(inputs["ffn_w_down"]),
        "cosT": cosT, "sinT": sinT, "tabs": tabs, "zz": zz,
    }
    x = f(inputs["x"])
    return [dict(shared, x=x[b]) for b in range(n)]


def kernel(**inputs):
    nc = Prog().build()
    in_maps = make_in_maps(inputs)
    res = run_bass_kernel_spmd(nc, in_maps, core_ids=list(range(NCORES)))
    return np.stack([np.asarray(r["out"], dtype=np.float32) for r in res.results], axis=0)
```
